# Optimizing a Trainium2 kernel written in Bass

```python
import math
import jax
import jax.numpy as jnp
from jax import lax
import numpy as np

D_MODEL = 2048
BATCH = 4
SEQ = 4096
DEPTH = 4

N_A_LAYERS = DEPTH // 2
N_B_LAYERS = DEPTH - N_A_LAYERS
RMS_EPS = 1e-6

GDN_K_DIM = 128
GDN_V_DIM = 128
GDN_K_HEADS = D_MODEL // GDN_K_DIM
GDN_V_HEADS = 2 * GDN_K_HEADS
GDN_QK_WIDTH = GDN_K_HEADS * GDN_K_DIM
GDN_V_WIDTH = GDN_V_HEADS * GDN_V_DIM
GDN_CONV_CH = 2 * GDN_QK_WIDTH + GDN_V_WIDTH
GDN_IN_WIDTH = GDN_CONV_CH + GDN_V_WIDTH + 2 * GDN_V_HEADS
CONV_WIDTH = 4
CHUNK = 64

DIL_HEAD_DIM = 128
DIL_HEADS = D_MODEL // DIL_HEAD_DIM
DILATION_GROUPS = ((128, 1), (512, 4), (2048, 16))
N_GROUPS = len(DILATION_GROUPS)
DIL_WIDTH = DIL_HEADS * DIL_HEAD_DIM
DIL_Q_WIDTH = N_GROUPS * DIL_WIDTH
DIL_IN_WIDTH = DIL_Q_WIDTH + DIL_WIDTH
N_BUCKETS = 32
MAX_DISTANCE = 2048

kernel_name = "yoco_deltanet_dilated_swa_trunk"


def rmsnorm(x, gain):
    x32 = x.astype(jnp.float32)
    y = x32 * lax.rsqrt(jnp.mean(x32 * x32, axis=-1, keepdims=True) + RMS_EPS)
    return y * gain.astype(jnp.float32)


def modulated_norm(x, gain, shift, scale):
    y = rmsnorm(x, gain) * (1.0 + scale[:, None, :].astype(jnp.float32)) + shift[:, None, :].astype(jnp.float32)
    return y.astype(x.dtype)


def l2norm(x, eps=1e-6):
    return x * lax.rsqrt(jnp.sum(x * x, axis=-1, keepdims=True) + eps)


def causal_depthwise_conv(x, w):
    k_width, ch = w.shape
    return lax.conv_general_dilated(
        x, w[:, None, :].astype(x.dtype), window_strides=(1,),
        padding=((k_width - 1, 0),), dimension_numbers=("NWC", "WIO", "NWC"),
        feature_group_count=ch)


def chunk_gated_delta_rule(q, k, v, g, beta):
    bsz, seq, heads, dk = q.shape
    dv = v.shape[-1]
    n_chunks = seq // CHUNK

    def blocks(t):
        t = jnp.moveaxis(t.astype(jnp.float32), 1, 2)
        return t.reshape(bsz, heads, n_chunks, CHUNK, *t.shape[3:])

    q, k, v, g, beta = blocks(q), blocks(k), blocks(v), blocks(g), blocks(beta)
    g = jnp.cumsum(g, axis=-1)
    causal = jnp.tril(jnp.ones((CHUNK, CHUNK), dtype=bool))
    strict = jnp.tril(jnp.ones((CHUNK, CHUNK), dtype=bool), -1)
    decay = jnp.exp(jnp.where(causal, g[..., :, None] - g[..., None, :], -jnp.inf))
    kb = k * beta[..., None]
    lmat = jnp.where(strict, jnp.einsum("bhnie,bhnje->bhnij", kb, k) * decay, 0.0)
    tmat = lmat + jnp.eye(CHUNK, dtype=jnp.float32)
    rhs = jnp.concatenate([v * beta[..., None], kb * jnp.exp(g)[..., None]], axis=-1)
    sol = lax.linalg.triangular_solve(tmat, rhs, left_side=True, lower=True,
                                      unit_diagonal=True)
    u, w = sol[..., :dv], sol[..., dv:]
    attn = jnp.einsum("bhnie,bhnje->bhnij", q, k) * decay
    g_last = g[..., -1]
    q_dec = q * jnp.exp(g)[..., None]
    k_dec = k * jnp.exp(g_last[..., None] - g)[..., None]

    xs = tuple(jnp.moveaxis(t, 2, 0) for t in (u, w, q_dec, k_dec, attn, g_last))

    def step(state, inp):
        u_n, w_n, qd_n, kd_n, at_n, gl_n = inp
        v_new = u_n - jnp.einsum("bhce,bhef->bhcf", w_n, state)
        o_n = (jnp.einsum("bhce,bhef->bhcf", qd_n, state)
               + jnp.einsum("bhij,bhjf->bhif", at_n, v_new))
        state = (state * jnp.exp(gl_n)[..., None, None]
                 + jnp.einsum("bhce,bhcf->bhef", kd_n, v_new))
        return state, o_n

    state0 = jnp.zeros((bsz, heads, dk, dv), jnp.float32)
    _, o = lax.scan(step, state0, xs)
    o = jnp.moveaxis(o, 0, 2).reshape(bsz, heads, seq, dv)
    return jnp.moveaxis(o, 1, 2)


def gated_deltanet_mixer(h, w_in, conv_w, a_log, dt_bias, o_gain, w_out):
    bsz, seq, _ = h.shape
    proj = h @ w_in.astype(h.dtype)
    qkv, z, b, a = jnp.split(
        proj, [GDN_CONV_CH, GDN_CONV_CH + GDN_V_WIDTH,
               GDN_CONV_CH + GDN_V_WIDTH + GDN_V_HEADS], axis=-1)
    qkv = jax.nn.silu(causal_depthwise_conv(qkv, conv_w)).astype(jnp.float32)
    q, k, v = jnp.split(qkv, [GDN_QK_WIDTH, 2 * GDN_QK_WIDTH], axis=-1)
    q = l2norm(q.reshape(bsz, seq, GDN_K_HEADS, GDN_K_DIM)) * (GDN_K_DIM ** -0.5)
    k = l2norm(k.reshape(bsz, seq, GDN_K_HEADS, GDN_K_DIM))
    v = v.reshape(bsz, seq, GDN_V_HEADS, GDN_V_DIM)
    rep = GDN_V_HEADS // GDN_K_HEADS
    q = jnp.repeat(q, rep, axis=2)
    k = jnp.repeat(k, rep, axis=2)
    beta = jax.nn.sigmoid(b.astype(jnp.float32))
    g = -jnp.exp(a_log.astype(jnp.float32)) * jax.nn.softplus(
        a.astype(jnp.float32) + dt_bias.astype(jnp.float32))
    o = chunk_gated_delta_rule(q, k, v, g, beta)
    o = o * lax.rsqrt(jnp.mean(o * o, axis=-1, keepdims=True) + RMS_EPS) * o_gain.astype(jnp.float32)
    o = o * jax.nn.silu(z.astype(jnp.float32).reshape(bsz, seq, GDN_V_HEADS, GDN_V_DIM))
    return o.reshape(bsz, seq, GDN_V_WIDTH).astype(h.dtype) @ w_out.astype(h.dtype)


def t5_bucket(dist):
    max_exact = N_BUCKETS // 2
    n = jnp.maximum(dist, 0)
    large = max_exact + (jnp.log(jnp.maximum(n, 1).astype(jnp.float32) / max_exact)
                         / math.log(MAX_DISTANCE / max_exact)
                         * (N_BUCKETS - max_exact)).astype(jnp.int32)
    large = jnp.minimum(large, N_BUCKETS - 1)
    return jnp.where(n < max_exact, n, large)


def dilated_group_attention(q, k, v, window, dilation, bias_table):
    bsz, seq, heads, dh = q.shape
    blk = window // dilation
    sub_len = seq // dilation
    n_blk = -(-sub_len // blk)
    sub_pad = n_blk * blk

    def strided(t):
        t = t.reshape(bsz, sub_len, dilation, heads, dh).transpose(0, 2, 3, 1, 4)
        t = jnp.pad(t, ((0, 0), (0, 0), (0, 0), (0, sub_pad - sub_len), (0, 0)))
        return t.reshape(bsz, dilation, heads, n_blk, blk, dh)

    def with_prev(t):
        prev = jnp.pad(t, ((0, 0), (0, 0), (0, 0), (1, 0), (0, 0), (0, 0)))[:, :, :, :-1]
        return jnp.concatenate([prev, t], axis=-2)

    qs = strided(q)
    kk = with_prev(strided(k))
    vv = with_prev(strided(v))
    qi = jnp.arange(blk)[:, None]
    kj = jnp.arange(2 * blk)[None, :]
    rel = qi + blk - kj
    blk_idx = jnp.arange(n_blk)[:, None, None]
    valid = (rel >= 0) & (rel <= blk) & (blk_idx * blk + kj - blk >= 0)
    bias = jnp.transpose(bias_table.astype(jnp.float32)[t5_bucket(rel * dilation)], (2, 0, 1))
    s = jnp.einsum("bdhnqe,bdhnke->bdhnqk", qs, kk) * (dh ** -0.5) + bias[:, None]
    s = jnp.where(valid, s, -jnp.inf)
    m = jnp.max(s, axis=-1, keepdims=True)
    p = jnp.exp(s - m)
    den = jnp.sum(p, axis=-1)
    o = jnp.einsum("bdhnqk,bdhnke->bdhnqe", p, vv) / den[..., None]
    lse = m[..., 0] + jnp.log(den)
    o = o.reshape(bsz, dilation, heads, sub_pad, dh)[:, :, :, :sub_len]
    o = o.transpose(0, 3, 1, 2, 4).reshape(bsz, seq, heads, dh)
    lse = lse.reshape(bsz, dilation, heads, sub_pad)[..., :sub_len]
    lse = lse.transpose(0, 3, 1, 2).reshape(bsz, seq, heads)
    return o, lse


def dilated_mixer(h, k_sh, v_sh, w_in, w_out, rel_bias):
    bsz, seq, _ = h.shape
    proj = h @ w_in.astype(h.dtype)
    q_all = proj[..., :DIL_Q_WIDTH].astype(jnp.float32)
    z = proj[..., DIL_Q_WIDTH:].astype(jnp.float32)
    outs, lses = [], []
    for gi, (window, dilation) in enumerate(DILATION_GROUPS):
        q = q_all[..., gi * DIL_WIDTH:(gi + 1) * DIL_WIDTH].reshape(bsz, seq, DIL_HEADS, DIL_HEAD_DIM)
        o, lse = dilated_group_attention(
            q, k_sh, v_sh, window, dilation,
            rel_bias[:, gi * DIL_HEADS:(gi + 1) * DIL_HEADS])
        outs.append(o)
        lses.append(lse)
    wts = jax.nn.softmax(jnp.stack(lses, axis=0), axis=0)
    o = jnp.sum(wts[..., None] * jnp.stack(outs, axis=0), axis=0)
    y = o.reshape(bsz, seq, DIL_WIDTH) * jax.nn.silu(z)
    return y.astype(h.dtype) @ w_out.astype(h.dtype)


def setup_inputs(seed: int = 0) -> dict:
    key = jax.random.key(seed)
    ks = jax.random.split(key, 20)

    def normal(k, shape, scale):
        return jax.random.normal(k, shape, jnp.float32) * scale

    dt = jnp.exp(jax.random.uniform(ks[8], (N_A_LAYERS, GDN_V_HEADS), jnp.float32,
                                    minval=math.log(1e-3), maxval=math.log(1e-1)))
    return {
        "x": normal(ks[0], (BATCH, SEQ, D_MODEL), 1.0),
        "c": normal(ks[1], (BATCH, D_MODEL), 1.0),
        "norm_gain": 1.0 + normal(ks[2], (DEPTH, D_MODEL), 0.02),
        "w_mod": normal(ks[3], (DEPTH, D_MODEL, 3 * D_MODEL), 0.5 * D_MODEL ** -0.5),
        "b_mod": normal(ks[4], (DEPTH, 3 * D_MODEL), 0.02),
        "w_in_a": normal(ks[5], (N_A_LAYERS, D_MODEL, GDN_IN_WIDTH), D_MODEL ** -0.5),
        "conv_w_a": normal(ks[6], (N_A_LAYERS, CONV_WIDTH, GDN_CONV_CH), CONV_WIDTH ** -0.5),
        "a_log": jnp.log(jax.random.uniform(ks[7], (N_A_LAYERS, GDN_V_HEADS), jnp.float32,
                                            minval=1.0, maxval=16.0)),
        "dt_bias": dt + jnp.log(-jnp.expm1(-dt)),
        "o_norm_a": 1.0 + normal(ks[9], (N_A_LAYERS, GDN_V_DIM), 0.02),
        "w_out_a": normal(ks[10], (N_A_LAYERS, GDN_V_WIDTH, D_MODEL), GDN_V_WIDTH ** -0.5),
        "kv_gain": 1.0 + normal(ks[11], (D_MODEL,), 0.02),
        "w_kv_mod": normal(ks[12], (D_MODEL, 2 * D_MODEL), 0.5 * D_MODEL ** -0.5),
        "b_kv_mod": normal(ks[13], (2 * D_MODEL,), 0.02),
        "w_kv": normal(ks[14], (D_MODEL, 2 * DIL_WIDTH), D_MODEL ** -0.5),
        "w_in_b": normal(ks[15], (N_B_LAYERS, D_MODEL, DIL_IN_WIDTH), D_MODEL ** -0.5),
        "w_out_b": normal(ks[16], (N_B_LAYERS, DIL_WIDTH, D_MODEL), DIL_WIDTH ** -0.5),
        "rel_bias": normal(ks[17], (N_BUCKETS, N_GROUPS * DIL_HEADS), 0.5),
        "final_gain": 1.0 + normal(ks[18], (D_MODEL,), 0.02),
    }


def reference(x, c, norm_gain, w_mod, b_mod, w_in_a, conv_w_a, a_log, dt_bias,
              o_norm_a, w_out_a, kv_gain, w_kv_mod, b_kv_mod, w_kv, w_in_b,
              w_out_b, rel_bias, final_gain):
    bsz, seq, _ = x.shape
    c_act = jax.nn.silu(c)
    mods = jnp.einsum("bd,lde->lbe", c_act, w_mod) + b_mod[:, None, :]
    k_sh = v_sh = None
    for layer in range(DEPTH):
        shift, scale, gate = jnp.split(mods[layer], 3, axis=-1)
        h = modulated_norm(x, norm_gain[layer], shift, scale)
        if layer < N_A_LAYERS:
            y = gated_deltanet_mixer(h, w_in_a[layer], conv_w_a[layer], a_log[layer],
                                     dt_bias[layer], o_norm_a[layer], w_out_a[layer])
        else:
            if layer == N_A_LAYERS:
                kv_shift, kv_scale = jnp.split(c_act @ w_kv_mod + b_kv_mod, 2, axis=-1)
                hk = modulated_norm(x, kv_gain, kv_shift, kv_scale)
                kv = (hk @ w_kv.astype(hk.dtype)).astype(jnp.float32)
                k_sh = kv[..., :DIL_WIDTH].reshape(bsz, seq, DIL_HEADS, DIL_HEAD_DIM)
                v_sh = kv[..., DIL_WIDTH:].reshape(bsz, seq, DIL_HEADS, DIL_HEAD_DIM)
            j = layer - N_A_LAYERS
            y = dilated_mixer(h, k_sh, v_sh, w_in_b[j], w_out_b[j], rel_bias)
        x = x + (gate[:, None, :] * y).astype(x.dtype)
    return rmsnorm(x, final_gain).astype(x.dtype)
```

```python
import numpy as np
import concourse.bass as bass
import concourse.mybir as mybir
from concourse.bass_utils import run_bass_kernel_spmd

F32 = mybir.dt.float32
BF16 = mybir.dt.bfloat16
ALU = mybir.AluOpType
AF = mybir.ActivationFunctionType
AX = mybir.AxisListType

SAME_ENG_RAW = True


class Buf:
    def __init__(self, ap, name=""):
        self.ap = ap
        self.name = name
        self.writers = {}
        self.readers = {}
        self.war = {}

    def __getitem__(self, idx):
        return View(self, self.ap[idx])

    def bitcast(self, dt):
        return View(self, self.ap.bitcast(dt))

    def rearrange(self, *a, **k):
        return View(self, self.ap.rearrange(*a, **k))


class View:
    def __init__(self, buf, ap):
        self.buf = buf
        self.ap = ap

    def __getitem__(self, idx):
        return View(self.buf, self.ap[idx])

    def rearrange(self, *a, **k):
        return View(self.buf, self.ap.rearrange(*a, **k))

    def bitcast(self, dt):
        return View(self.buf, self.ap.bitcast(dt))


def _ap(x):
    if isinstance(x, View):
        return x.ap
    if isinstance(x, Buf):
        return x.ap
    return x


def _bufs(xs):
    out = []
    for x in xs:
        if isinstance(x, View):
            out.append(x.buf)
        elif isinstance(x, Buf):
            out.append(x)
    return out


class Eng:
    def __init__(self, name, h, sem, si):
        self.name = name
        self.h = h
        self.sem = sem
        self.si = si
        self.cnt = 0
        self.waited = {}
        self.dma_i = 0
        self.dma_sems = []


class Prog:
    def __init__(self, nc, n_dma_sems=(40, 16, 24)):
        self.nc = nc
        self.sems = []
        self.engs = {}
        for name, h in [("pe", nc.tensor), ("dve", nc.vector), ("act", nc.scalar),
                        ("pool", nc.gpsimd), ("sp", nc.sync)]:
            s = nc.alloc_semaphore(name=f"sem_{name}")
            self.sems.append(s)
            self.engs[name] = Eng(name, h, s, len(self.sems) - 1)
        for qn, n in zip(("sp", "act", "pool"), n_dma_sems):
            e = self.engs[qn]
            for i in range(n):
                s = nc.alloc_semaphore(name=f"dsem_{qn}{i}")
                self.sems.append(s)
                e.dma_sems.append(len(self.sems) - 1)
        self.pe, self.dve, self.act, self.pool, self.sp = (
            self.engs[k] for k in ("pe", "dve", "act", "pool", "sp"))
        self._ctx = []
        self.n_wait = 0
        self.n_ins = 0

    def sbuf(self, shape, dt, name):
        self._uid = getattr(self, "_uid", 0) + 1
        name = f"{name}_u{self._uid}"
        cm = self.nc.sbuf_tensor(name, list(shape), dt)
        t = cm.__enter__()
        self._ctx.append(cm)
        return Buf(t.ap() if hasattr(t, "ap") else t[:], name)

    def psum(self, shape, dt, name):
        cm = self.nc.psum_tensor(name, list(shape), dt)
        t = cm.__enter__()
        self._ctx.append(cm)
        b = Buf(t.ap() if hasattr(t, "ap") else t[:], name)
        b.excl = True
        return b

    def dram(self, name, shape, dt, kind="Internal"):
        t = self.nc.dram_tensor(name, list(shape), dt, kind=kind)
        return Buf(t.ap(), name)

    def close(self):
        for cm in reversed(self._ctx):
            cm.__exit__(None, None, None)
        self._ctx = []

    def _wait(self, eng, si, val):
        if eng.waited.get(si, 0) >= val:
            return
        eng.h.wait_ge(self.sems[si], val)
        eng.waited[si] = val
        self.n_wait += 1

    def _sync(self, eng, reads, writes):
        deps = {}
        for b in reads:
            for si, v in b.writers.items():
                if si == eng.si and (eng.name == "pe" or not SAME_ENG_RAW):
                    continue
                if deps.get(si, 0) < v:
                    deps[si] = v
            if getattr(b, "excl", False):
                for si, v in b.readers.items():
                    if si == eng.si:
                        continue
                    if deps.get(si, 0) < v:
                        deps[si] = v
        for b in writes:
            for d in (b.writers, b.readers, b.war):
                for si, v in d.items():
                    if si == eng.si:
                        continue
                    if deps.get(si, 0) < v:
                        deps[si] = v
        for si, v in deps.items():
            self._wait(eng, si, v)

    def _mark(self, tok, reads, writes):
        si, v = tok
        for b in writes:
            if b.readers:
                b.war = b.readers
                b.readers = {}
                b.writers = {}
            b.writers[si] = max(b.writers.get(si, 0), v)
        wset = set(id(b) for b in writes)
        for b in reads:
            if id(b) in wset:
                continue
            b.readers[si] = max(b.readers.get(si, 0), v)

    def op(self, eng, fn, *, outs, ins, **kw):
        rb, wb = _bufs(ins), _bufs(outs)
        self._sync(eng, rb, wb)
        inst = fn(**kw)
        eng.cnt += 1
        inst.then_inc(eng.sem, 1)
        self._mark((eng.si, eng.cnt), rb, wb)
        self.n_ins += 1
        return inst

    def dma(self, q, out, in_, **kw):
        rb, wb = _bufs([in_]), _bufs([out])
        self._sync_dma(q, rb, wb)
        k = len(q.dma_sems)
        si = q.dma_sems[q.dma_i % k]
        val = 16 * (q.dma_i // k + 1)
        if val > 16:
            self._wait(q, si, val - 16)
        q.dma_i += 1
        inst = q.h.dma_start(out=_ap(out), in_=_ap(in_), **kw)
        inst.then_inc(self.sems[si], 16)
        self._mark((si, val), rb, wb)
        self.n_ins += 1
        return (si, val)

    def _sync_dma(self, eng, reads, writes):
        deps = {}
        dma_sis = set()
        for e in self.engs.values():
            dma_sis.update(e.dma_sems)
        for b in reads:
            for si, v in b.writers.items():
                if deps.get(si, 0) < v:
                    deps[si] = v
        for b in writes:
            for si, v in b.readers.items():
                if deps.get(si, 0) < v:
                    deps[si] = v
            for si, v in b.war.items():
                if deps.get(si, 0) < v:
                    deps[si] = v
            for si, v in b.writers.items():
                if si in dma_sis:
                    continue
                if deps.get(si, 0) < v:
                    deps[si] = v
        for si, v in deps.items():
            self._wait(eng, si, v)

    def wait_all(self, eng, bufs):
        for b in bufs:
            for d in (b.writers, b.readers):
                for si, v in d.items():
                    self._wait(eng, si, v)

    def matmul(self, out, lhsT, rhs, start=True, stop=True, **kw):
        return self.op(self.pe, self.nc.tensor.matmul, outs=[out], ins=[lhsT, rhs],
                       out=_ap(out), lhsT=_ap(lhsT), rhs=_ap(rhs), start=start, stop=stop, **kw)

    def transpose(self, out, in_, ident):
        return self.op(self.pe, self.nc.tensor.transpose, outs=[out], ins=[in_, ident],
                       out=_ap(out), in_=_ap(in_), identity=_ap(ident))

    def _e(self, eng):
        return eng if isinstance(eng, Eng) else self.engs[eng]

    def activation(self, out, in_, func, bias=None, scale=None, accum_out=None, eng="act"):
        e = self._e(eng)
        ins = [in_]
        kw = dict(out=_ap(out), in_=_ap(in_), func=func)
        if bias is not None:
            kw["bias"] = _ap(bias)
            ins.append(bias)
        if scale is not None:
            kw["scale"] = _ap(scale)
            ins.append(scale)
        outs = [out]
        if accum_out is not None:
            kw["accum_out"] = _ap(accum_out)
            outs.append(accum_out)
        return self.op(e, e.h.activation, outs=outs, ins=ins, **kw)

    def tensor_scalar(self, eng, out, in0, s1, s2=None, op0=ALU.mult, op1=None, accum_out=None):
        e = self._e(eng)
        ins = [in0, s1, s2]
        kw = dict(out=_ap(out), in0=_ap(in0), scalar1=_ap(s1), scalar2=_ap(s2), op0=op0)
        if op1 is not None:
            kw["op1"] = op1
        outs = [out]
        if accum_out is not None:
            kw["accum_out"] = _ap(accum_out)
            outs.append(accum_out)
        return self.op(e, e.h.tensor_scalar, outs=outs, ins=ins, **kw)

    def tensor_tensor(self, eng, out, in0, in1, op):
        e = self._e(eng)
        return self.op(e, e.h.tensor_tensor, outs=[out], ins=[in0, in1],
                       out=_ap(out), in0=_ap(in0), in1=_ap(in1), op=op)

    def stt(self, out, in0, scalar, in1, op0, op1, eng="dve", accum_out=None):
        e = self._e(eng)
        kw = dict(out=_ap(out), in0=_ap(in0), scalar=_ap(scalar), in1=_ap(in1), op0=op0, op1=op1)
        outs = [out]
        if accum_out is not None:
            kw["accum_out"] = _ap(accum_out)
            outs.append(accum_out)
        return self.op(e, e.h.scalar_tensor_tensor, outs=outs, ins=[in0, scalar, in1], **kw)

    def copy(self, eng, out, in_):
        e = self._e(eng)
        if e.name == "act":
            return self.op(e, e.h.activation, outs=[out], ins=[in_], out=_ap(out), in_=_ap(in_),
                           func=AF.Copy)
        return self.op(e, e.h.tensor_copy, outs=[out], ins=[in_], out=_ap(out), in_=_ap(in_))

    def memset(self, eng, out, val):
        e = self._e(eng)
        return self.op(e, e.h.memset, outs=[out], ins=[], ap=_ap(out), constant=val)

    def reduce(self, eng, out, in_, op, axis=AX.X):
        e = self._e(eng)
        return self.op(e, e.h.tensor_reduce, outs=[out], ins=[in_], out=_ap(out), in_=_ap(in_),
                       axis=axis, op=op)

    def reciprocal(self, out, in_):
        e = self.dve
        return self.op(e, e.h.reciprocal, outs=[out], ins=[in_], out=_ap(out), in_=_ap(in_))

    def barrier(self):
        self.finish()
        inst = self.sp.h.nop()
        self.sp.cnt += 1
        inst.then_inc(self.sp.sem, 1)
        for e in self.engs.values():
            if e is not self.sp:
                self._wait(e, self.sp.si, self.sp.cnt)

    def finish(self):
        for e in self.engs.values():
            if e.cnt:
                self._wait(self.sp, e.si, e.cnt)
            k = len(e.dma_sems)
            for j in range(min(k, e.dma_i)):
                n_uses = (e.dma_i - 1 - j) // k + 1
                self._wait(self.sp, e.dma_sems[j], 16 * n_uses)


import os
STOP = int(os.environ.get('GDN_STOP', '99'))

D = 2048
S = 4096
NT = 512
NTT = S // NT
NCH = 16
EPS = 1e-6


class Rot:
    def __init__(self, p, name, shape, dt, n=2):
        self.bufs = [p.sbuf(shape, dt, f"{name}{i}") for i in range(n)]
        self.i = 0

    def next(self):
        b = self.bufs[self.i % len(self.bufs)]
        self.i += 1
        return b


class KB:
    def __init__(self, nc):
        self.nc = nc
        self.p = Prog(nc)
        p = self.p
        self.psb = [p.psum([128, 512], F32, f"ps{i}") for i in range(8)]
        self.psi = 0
        self.marks = []
        self.din = {}
        self.dq = 0

    def ps(self):
        b = self.psb[self.psi % 8]
        self.psi += 1
        return b

    def ext_in(self, name, shape, dt=F32):
        t = self.nc.dram_tensor(name, list(shape), dt, kind="ExternalInput")
        b = Buf(t.ap(), name)
        self.din[name] = b
        return b

    def ext_out(self, name, shape, dt=F32):
        t = self.nc.dram_tensor(name, list(shape), dt, kind="ExternalOutput")
        return Buf(t.ap(), name)

    def push(self):
        self.marks.append(len(self.p._ctx))

    def pop(self):
        m = self.marks.pop()
        p = self.p
        p.barrier()
        while len(p._ctx) > m:
            cm = p._ctx.pop()
            cm.__exit__(None, None, None)

    def dump(self, idx, view, w=128):
        d = getattr(self, "dbg_d", None)
        if d is None:
            return
        self.p.dma(self.p.pool, d[:, idx, 0:w], view)

    def ldq(self):
        return self.p.sp

    def load_consts2(self, consts2_d):
        p = self.p
        self.bmk = p.sbuf([128, 4, 256], BF16, "bmk")
        self.push()
        cf2 = p.sbuf([128, 1024], F32, "cf2")
        p.dma(p.sp, cf2, consts2_d)
        p.copy("dve", self.bmk, cf2[:, :].rearrange("p (a b) -> p a b", b=256))
        self.pop()

    def load_consts(self, consts_d):
        p = self.p
        self.cf = p.sbuf([128, 5 * 128], F32, "cf")
        p.dma(p.sp, self.cf, consts_d)
        self.ident_f = self.cf[:, 0:128]
        self.ones_f = self.cf[:, 128:256]
        self.U_f = self.cf[:, 256:384]
        self.masks = self.cf[:, 384:640]
        self.cb = p.sbuf([128, 256], BF16, "cb")
        p.copy("dve", self.cb, self.cf[:, 0:256])
        self.ident_b = self.cb[:, 0:128]
        self.ones_b = self.cb[:, 128:256]
        self.cc = p.sbuf([128, 4], F32, "cc")
        p.memset("dve", self.cc[:, 0:1], EPS)
        p.memset("dve", self.cc[:, 1:2], 1.0)
        p.memset("dve", self.cc[:, 2:3], 0.0)
        p.memset("dve", self.cc[:, 3:4], float(np.log(128.0 ** -0.5)))
        self.eps_c = self.cc[:, 0:1]
        self.one_c = self.cc[:, 1:2]
        self.zero_c = self.cc[:, 2:3]
        self.lnq_c = self.cc[:, 3:4]

    def mods_phase(self, cT_d, wmod_d, bmodT_d, ncols, out_sb):
        p = self.p
        self.push()
        nj = ncols // 128
        cT = p.sbuf([128, 16], F32, "cT")
        cact = p.sbuf([128, 16], F32, "cact")
        bm = p.sbuf([128, nj], F32, "bm")
        p.dma(p.sp, cT, cT_d)
        p.dma(p.sp, bm, bmodT_d)
        p.activation(cact, cT, AF.Silu)
        wr = Rot(p, "wm", [128, 16, 512], F32, 2)
        wv = wmod_d.ap.rearrange("(c p) n -> p c n", p=128)
        ps = self.ps()
        for blk in range(ncols // 512):
            w = wr.next()
            for q4 in range(4):
                p.dma(self.ldq(), w[:, q4 * 4:(q4 + 1) * 4, :],
                      View(wmod_d, wv[:, q4 * 4:(q4 + 1) * 4, blk * 512:(blk + 1) * 512]))
            for jj in range(4):
                j = blk * 4 + jj
                for c in range(16):
                    p.matmul(ps[:, j:j + 1], w[:, c, jj * 128:(jj + 1) * 128], cact[:, c:c + 1],
                             start=(c == 0), stop=(c == 15))
        p.tensor_tensor("dve", out_sb, ps[:, 0:nj], bm, ALU.add)
        self.pop()

    def norm_phase(self, xT_d, hT_d, Acol, Scol, wba_d=None, ba_sb=None):
        p = self.p
        self.push()
        xr = Rot(p, "nx", [128, 16, NT], F32, 2)
        hr = Rot(p, "nh", [128, 16, NT], BF16, 2)
        sr = Rot(p, "nsq", [128, NT], F32, 3)
        tr = Rot(p, "ntm", [128, NT], F32, 3)
        rr = Rot(p, "nrs", [128, NT], F32, 2)
        if wba_d is not None:
            wba = p.sbuf([128, 16, 32], BF16, "wba")
            p.dma(p.pool, wba, View(wba_d, wba_d.ap.rearrange("(c p) n -> p c n", p=128)))
        for tt in range(NTT):
            t0 = tt * NT
            x_t = xr.next()
            for q4 in range(4):
                p.dma(self.ldq(), x_t[:, q4 * 4:(q4 + 1) * 4, :],
                      View(xT_d, xT_d.ap[tt][:, q4 * 4:(q4 + 1) * 4, :]))
            ps = self.ps()
            for c in range(16):
                s = sr.next()
                p.tensor_tensor("pool", s, x_t[:, c, :], x_t[:, c, :], ALU.mult)
                p.matmul(ps, self.ones_f, s, start=(c == 0), stop=(c == 15))
            rstd = rr.next()
            p.activation(rstd, ps, AF.Ln, scale=1.0 / D, bias=self.eps_c)
            p.activation(rstd, rstd, AF.Exp, scale=-0.5)
            h_t = hr.next()
            for c in range(16):
                tmp = tr.next()
                p.tensor_tensor("dve", tmp, x_t[:, c, :], rstd, ALU.mult)
                p.activation(h_t[:, c, :], tmp, AF.Identity, scale=Acol[:, c:c + 1], bias=Scol[:, c:c + 1])
            for q4 in range(2):
                p.dma(p.sp, View(hT_d, hT_d.ap[tt][:, q4 * 8:(q4 + 1) * 8, :]),
                      h_t[:, q4 * 8:(q4 + 1) * 8, :])
            if wba_d is not None:
                pb = self.ps()
                for ch in range(4):
                    for half in range(2):
                        for c in range(16):
                            p.matmul(pb[:, half * 64 + ch * 16:half * 64 + (ch + 1) * 16],
                                     h_t[:, c, ch * 128:(ch + 1) * 128],
                                     wba[:, c, half * 16:(half + 1) * 16], start=(c == 0), stop=(c == 15))
                for half in range(2):
                    p.copy("dve", ba_sb[:, half, tt * 64:(tt + 1) * 64], pb[:, half * 64:(half + 1) * 64])
        self.pop()

    def outproj_phase(self, oT_d, nk, wout_d, xT_d, xTo_d, gate_col, t_lo=0, t_hi=NTT,
                      final_gain=None):
        p = self.p
        nh = 2 if nk == 32 else 1
        CW = D // nh
        NCc = 16 // nh
        assert final_gain is None or nh == 1
        wv = wout_d.ap.rearrange("(k p) n -> p k n", p=128)
        for half in range(nh):
            self.push()
            w = p.sbuf([128, nk, CW], BF16, "wo")
            for k in range(nk):
                p.dma(p.pool, w[:, k, :], View(wout_d, wv[:, k, half * CW:(half + 1) * CW]))
            orr = Rot(p, "oo", [128, nk, NT], BF16, 2)
            xr = Rot(p, "ox", [128, NCc, NT], F32, 2)
            sr = Rot(p, "osq", [128, NT], F32, 3)
            for tt in range(t_lo, t_hi):
                t0 = tt * NT
                o_t = orr.next()
                nq = max(1, nk // 4)
                for q4 in range(nk // nq):
                    p.dma(self.ldq(), o_t[:, q4 * nq:(q4 + 1) * nq, :],
                          View(oT_d, oT_d.ap[tt][:, q4 * nq:(q4 + 1) * nq, :]))
                x_t = xr.next()
                c0 = half * NCc
                for q4 in range(2):
                    h2 = NCc // 2
                    p.dma(self.ldq(), x_t[:, q4 * h2:(q4 + 1) * h2, :],
                          View(xT_d, xT_d.ap[tt][:, c0 + q4 * h2:c0 + (q4 + 1) * h2, :]))
                for c in range(NCc):
                    ps = self.ps()
                    for k in range(nk):
                        p.matmul(ps, w[:, k, c * 128:(c + 1) * 128], o_t[:, k, :],
                                 start=(k == 0), stop=(k == nk - 1))
                    p.stt(x_t[:, c, :], ps, gate_col[:, c0 + c:c0 + c + 1], x_t[:, c, :], ALU.mult, ALU.add)
                if final_gain is not None:
                    pss = self.ps()
                    for c in range(16):
                        s = sr.next()
                        p.tensor_tensor("pool", s, x_t[:, c, :], x_t[:, c, :], ALU.mult)
                        p.matmul(pss, self.ones_f, s, start=(c == 0), stop=(c == 15))
                    rstd = sr.next()
                    p.activation(rstd, pss, AF.Ln, scale=1.0 / D, bias=self.eps_c)
                    p.activation(rstd, rstd, AF.Exp, scale=-0.5)
                    for c in range(16):
                        p.stt(x_t[:, c, :], x_t[:, c, :], final_gain[:, c:c + 1], rstd, ALU.mult, ALU.mult)
                for q4 in range(2):
                    h2 = NCc // 2
                    p.dma(p.sp, View(xTo_d, xTo_d.ap[tt][:, c0 + q4 * h2:c0 + (q4 + 1) * h2, :]),
                          x_t[:, q4 * h2:(q4 + 1) * h2, :])
            self.pop()


class T_:
    pass


def _gb_phase(self, ba_sb, alog_d, dtb_d):
    p = self.p
    T = T_()
    mk = lambda nm: p.sbuf([128, 512], F32, nm)
    T.beta, T.g, T.gc, T.egc, T.ekd, T.egl = (mk(n) for n in ("beta", "g", "gc", "egc", "ekd", "egl"))
    self.push()
    al, dt, x, ax, e = (mk(n) for n in ("al", "dt", "gx", "gax", "ge"))
    p.dma(p.sp, al, alog_d)
    p.dma(p.sp, dt, dtb_d)
    bv = ba_sb[:, 0, :]
    av = ba_sb[:, 1, :]
    v3 = lambda t: t
    p.activation(v3(e), bv, AF.Exp, scale=-1.0)
    p.tensor_scalar("dve", e, e, 1.0, None, op0=ALU.add)
    p.reciprocal(T.beta, e)
    GB = int(os.environ.get('GB', '9'))
    if GB <= 1:
        self.pop(); return T
    p.tensor_tensor("dve", v3(x), av, v3(dt), ALU.add)
    p.stt(ax, x, -1.0, x, ALU.mult, ALU.max)
    p.activation(e, ax, AF.Exp, scale=-1.0)
    p.activation(e, e, AF.Ln, bias=self.one_c)
    p.tensor_scalar("dve", ax, x, 0.0, None, op0=ALU.max)
    p.tensor_tensor("dve", x, ax, e, ALU.add)
    if GB <= 2:
        self.pop(); return T
    p.activation(al, al, AF.Exp)
    p.stt(T.g, x, -1.0, al, ALU.mult, ALU.mult)
    if GB <= 3:
        self.pop(); return T
    ps = self.ps()
    p.matmul(ps, self.U_f, T.g)
    p.copy("dve", T.gc, ps)
    if GB <= 4:
        self.pop(); return T
    p.activation(T.egc, ps, AF.Exp)
    if GB <= 5:
        self.pop(); return T
    ps2 = self.ps()
    p.matmul(ps2, self.ones_f, T.g)
    p.activation(T.egl, ps2, AF.Exp)
    if GB <= 6:
        self.pop(); return T
    p.tensor_tensor("dve", x, ps2, T.gc, ALU.subtract)
    p.activation(T.ekd, x, AF.Exp)
    self.pop()
    return T


def _gdn_phase(self, hT_d, win_d, convw_d, ogain_d, oT_d, T, ngroups=8, ntt=NTT):
    p = self.p
    self.push()
    cw = p.sbuf([128, 128], F32, "cw")
    og = p.sbuf([128, 1], F32, "og")
    p.dma(p.sp, cw, convw_d)
    p.dma(p.sp, og, ogain_d)
    ident2 = p.sbuf([128, 2, 128], F32, "ident2")
    p.copy("dve", ident2[:, 0, :], self.ident_f)
    p.copy("dve", ident2[:, 1, :], self.ident_f)
    wr = Rot(p, "gw", [128, 16, 768], BF16, 1)
    hr = Rot(p, "gh", [128, 16, NT], BF16, 2)
    xinr = Rot(p, "gxin", [128, 4, NT + 3], F32, 2)
    cvr = Rot(p, "gcv", [128, 4, NT], F32, 1)
    zsr = Rot(p, "gzs", [128, 2, NT], F32, 3)
    sqr = Rot(p, "gsq", [128, 2, NT], F32, 1)
    rsr = Rot(p, "grs", [128, NT], F32, 2)
    qbr = Rot(p, "gqb", [128, NT], BF16, 1)
    kbr = Rot(p, "gkb", [128, NT], BF16, 1)
    kfr = Rot(p, "gkf", [128, NT], F32, 1)
    otr = Rot(p, "got", [128, 2, NT], F32, 1)
    obr = Rot(p, "gob", [128, 2, NT], BF16, 1)
    HB = [[{nm: p.sbuf([128, 2, 128], BF16, f"hb{pp_}{cc_}{nm}") for nm in ("X", "vt", "nw", "kd", "attn", "qd")}
           for cc_ in range(4)] for pp_ in range(2)]
    RS = []
    for ch_ in range(4):
        R = T_()
        for nm, dt in (("kg", BF16), ("ug", F32), ("dd", F32), ("egr", F32),
                       ("Nb", BF16), ("Y", BF16), ("Nt", BF16), ("Pt", BF16),
                       ("P", BF16), ("vn", BF16), ("Et", BF16)):
            setattr(R, nm, Rot(p, f"c{ch_}" + nm, [128, 2, 128], dt, 3 if nm in ("Y", "P", "Pt", "Et") else 1))
        R.kkq = Rot(p, f"c{ch_}kkq", [128, 256], F32, 1)
        RS.append(R)
    Sf = p.sbuf([128, 2, 128], F32, "Sf")
    Sb = p.sbuf([128, 2, 128], BF16, "Sb")
    f2 = lambda v: v.rearrange("p (a b) -> p a b", b=128)
    M, A, SUB, MIN = ALU.mult, ALU.add, ALU.subtract, ALU.min

    for g in range(ngroups):
        if STOP < 0:
            break
        w = wr.next()
        wv = win_d.ap[g].rearrange("(c p) n -> p c n", p=128)
        for q4 in range(4):
            p.dma(p.pool, w[:, q4 * 4:(q4 + 1) * 4, :], View(win_d, wv[:, q4 * 4:(q4 + 1) * 4, :]))
        p.memset("pool", Sf, 0.0)
        p.memset("pool", Sb, 0.0)
        pending = None
        hbuf = {}
        F1out = {}
        fstate = {"xin_prev": None}

        def load_h(tt_):
            h_t = hr.next()
            for q4 in range(2):
                p.dma(self.ldq(), h_t[:, q4 * 8:(q4 + 1) * 8, :],
                      View(hT_d, hT_d.ap[tt_][:, q4 * 8:(q4 + 1) * 8, :]))
            hbuf[tt_] = h_t

        def front1(tt_):
            h_t = hbuf.pop(tt_)
            xin = xinr.next()
            zs_ = zsr.next()
            F1out[tt_] = (xin, zs_)
            for blk in range(6):
                ps = self.ps()
                for c in range(16):
                    p.matmul(ps, w[:, c, blk * 128:(blk + 1) * 128], h_t[:, c, :],
                             start=(c == 0), stop=(c == 15))
                if blk < 4:
                    p.copy("act", xin[:, blk, 3:3 + NT], ps)
                else:
                    p.activation(zs_[:, blk - 4, :], ps, AF.Silu)
                yield
            if fstate["xin_prev"] is None:
                p.memset("pool", xin[:, :, 0:3], 0.0)
            else:
                p.copy("pool", xin[:, :, 0:3], fstate["xin_prev"][:, :, NT:NT + 3])
            fstate["xin_prev"] = xin
            yield

        load_h(0)
        if ntt > 1:
            load_h(1)
        for _ in front1(0):
            pass
        for tt in range(ntt):
            t0g = tt * NT
            if tt + 2 < ntt:
                load_h(tt + 2)
            xin, zs = F1out.pop(tt)
            cv = cvr.next()
            for blk in range(4):
                cb = (g * 4 + blk) * 4
                p.tensor_scalar("dve", cv[:, blk, :], xin[:, blk, 0:NT], cw[:, cb:cb + 1], None, op0=M)
                for j in range(1, 4):
                    p.stt(cv[:, blk, :], xin[:, blk, j:j + NT], cw[:, cb + j:cb + j + 1], cv[:, blk, :], M, A)
            p.activation(cv, cv, AF.Silu)
            sq = sqr.next()
            p.activation(sq, cv[:, 0:2, :], AF.Square)
            qb, kb, kf = qbr.next(), kbr.next(), kfr.next()
            for blk in range(2):
                pn = self.ps()
                p.matmul(pn, self.ones_f, sq[:, blk, :])
                rs = rsr.next()
                p.activation(rs, pn, AF.Ln, bias=self.eps_c)
                if blk == 0:
                    p.activation(rs, rs, AF.Exp, scale=-0.5, bias=self.lnq_c)
                    p.tensor_tensor("dve", qb, cv[:, 0, :], rs, M)
                else:
                    p.activation(rs, rs, AF.Exp, scale=-0.5)
                    p.tensor_tensor("dve", kf, cv[:, 1, :], rs, M)
                    p.copy("act", kb, kf)
            o_t = otr.next()
            if STOP <= 2:
                continue
            J = lambda ps_, j: ps_[:, j * 128:(j + 1) * 128]
            f3 = lambda v: v.rearrange("p (a b) -> p a b", b=128)
            mk = lambda i: f3(self.bmk[:, i, :])
            par = tt % 2
            hand = {}

            def stageA(ch, par=par, tt=tt, kf=kf, cv=cv, kb=kb, qb=qb, hand=hand):
                n = tt * 4 + ch
                sl = slice(ch * 128, (ch + 1) * 128)
                cjs = [n * 16 + 2 * g + j for j in (0, 1)]
                col = lambda t, j: t[:, cjs[j]:cjs[j] + 1]
                R = RS[ch]
                pT = self.ps()
                p.transpose(pT[:, 0:128], kf[:, sl], self.ident_f)
                p.transpose(pT[:, 128:256], cv[:, 2, sl], self.ident_f)
                p.transpose(pT[:, 256:384], cv[:, 3, sl], self.ident_f)
                HBc = HB[par][ch]
                kg, kd, vt = R.kg.next(), HBc['kd'], HBc['vt']
                for j in range(2):
                    p.tensor_scalar("dve", kg[:, j, :], pT[:, 0:128], col(T.egc, j), None, op0=M)
                    p.activation(kd[:, j, :], pT[:, 0:128], AF.Identity, scale=col(T.ekd, j))
                p.copy("act", vt, f2(pT[:, 128:384]))
                pk = self.ps()
                p.matmul(pk[:, 0:128], kb[:, sl], kb[:, sl])
                p.matmul(pk[:, 128:256], kb[:, sl], qb[:, sl])
                kkq = R.kkq.next()
                p.tensor_tensor("dve", kkq, pk[:, 0:256], self.masks, M)
                ug = R.ug.next()
                for j in range(2):
                    p.activation(ug[:, j, :], self.U_f, AF.Identity, scale=col(T.g, j))
                yield
                pg = self.ps()
                for j in range(2):
                    p.matmul(J(pg, j), self.ones_f, ug[:, j, :])
                dd, egr = R.dd.next(), R.egr.next()
                for j in range(2):
                    p.tensor_scalar("dve", dd[:, j, :], J(pg, j), col(T.gc, j), 0.0, op0=SUB, op1=MIN)
                p.activation(egr, f2(pg[:, 0:256]), AF.Exp)
                yield
                p.activation(dd, dd, AF.Exp)
                yield
                Nb = R.Nb.next()
                for j in range(2):
                    p.stt(Nb[:, j, :], kkq[:, 0:128], col(T.beta, j), dd[:, j, :], M, M)
                attn, qd = HBc['attn'], HBc['qd']
                for j in range(2):
                    p.tensor_tensor("pool", attn[:, j, :], kkq[:, 128:256], dd[:, j, :], M)
                    p.tensor_tensor("pool", qd[:, j, :], qb[:, sl], egr[:, j, :], M)
                yield
                pt = self.ps().bitcast(BF16)
                for j in range(2):
                    p.transpose(J(pt, j), Nb[:, j, :], self.ident_b)
                Ntb = R.Nt.next()
                p.copy("act", Ntb, f2(pt[:, 0:256]))
                N16 = R.P.next()
                p.tensor_tensor("pool", N16, Nb, mk(0), M)
                Y = R.Y.next()
                p.stt(Y, N16, -1.0, ident2, M, A)
                yield
                N16t = R.Pt.next()
                p.tensor_tensor("pool", N16t, Ntb, mk(0), M)
                Ets = []
                for lv in range(3):
                    et = R.Et.next()
                    p.tensor_tensor("pool", et, Ntb, mk(1 + lv), M)
                    Ets.append(et)
                yield
                P_, Pt_ = N16, N16t

                def lvl_a(k, P_, Pt_):
                    pp = self.ps()
                    for j in range(2):
                        p.matmul(J(pp, j), P_[:, j, :], Pt_[:, j, :])
                    Ptn = R.Pt.next()
                    p.copy("act", Ptn, f2(pp[:, 0:256]))
                    Pn = None
                    if k < 3:
                        pq = self.ps()
                        for j in range(2):
                            p.matmul(J(pq, j), Pt_[:, j, :], P_[:, j, :])
                        Pn = R.P.next()
                        p.copy("dve", Pn, f2(pq[:, 0:256]))
                    return Pn, Ptn

                def lvl_b(Ptn, Y):
                    py = self.ps()
                    for j in range(2):
                        p.matmul(J(py, j), Ptn[:, j, :], Y[:, j, :])
                    Yn = R.Y.next()
                    p.tensor_tensor("dve", Yn, f2(py[:, 0:256]), Y, A)
                    return Yn

                Pn, Ptn = lvl_a(1, P_, Pt_)
                yield
                for k in range(1, 4):
                    Y = lvl_b(Ptn, Y)
                    if k < 3:
                        Pn, Ptn = lvl_a(k + 1, Pn, Ptn)
                    yield
                for lv in range(3):
                    ptb = self.ps().bitcast(BF16)
                    for j in range(2):
                        p.transpose(J(ptb, j), Y[:, j, :], self.ident_b)
                    Bt = R.Pt.next()
                    p.copy("act", Bt, f2(ptb[:, 0:256]))
                    pz = self.ps()
                    for j in range(2):
                        p.matmul(J(pz, j), Ets[lv][:, j, :], Y[:, j, :])
                    Z = R.P.next()
                    p.copy("dve", Z, f2(pz[:, 0:256]))
                    yield
                    pw = self.ps()
                    for j in range(2):
                        p.matmul(J(pw, j), Bt[:, j, :], Z[:, j, :])
                    Yn = HBc['X'] if lv == 2 else R.Y.next()
                    p.tensor_tensor("dve", Yn, Y, f2(pw[:, 0:256]), SUB)
                    Y = Yn
                    yield
                X = Y
                pw = self.ps()
                for j in range(2):
                    p.matmul(J(pw, j), kg[:, j, :], X[:, j, :])
                nw = HBc['nw']
                p.activation(nw, f2(pw[:, 0:256]), AF.Copy, scale=-1.0)
                hand[ch] = (X, vt, nw, kd, attn, qd)
                yield

            def stageB(tt=tt, hand=hand, o_t=o_t):
                for ch in range(4):
                    n = tt * 4 + ch
                    sl = slice(ch * 128, (ch + 1) * 128)
                    cjs = [n * 16 + 2 * g + j for j in (0, 1)]
                    col = lambda t, j: t[:, cjs[j]:cjs[j] + 1]
                    X, vt, nw, kd, attn, qd = hand[ch]
                    pv = self.ps()
                    for j in range(2):
                        p.matmul(J(pv, j), X[:, j, :], vt[:, j, :], start=True, stop=False)
                        p.matmul(J(pv, j), nw[:, j, :], Sb[:, j, :], start=False, stop=True)
                    vn = RS[ch].vn.next()
                    for j in range(2):
                        p.tensor_scalar("dve", vn[:, j, :], J(pv, j), col(T.beta, j), None, op0=M)
                    yield
                    po = self.ps()
                    pS = self.ps()
                    for j in range(2):
                        p.matmul(J(pS, j), kd[:, j, :], vn[:, j, :])
                    for j in range(2):
                        p.matmul(J(po, j), Sb[:, j, :], qd[:, j, :], start=True, stop=False)
                        p.matmul(J(po, j), vn[:, j, :], attn[:, j, :], start=False, stop=True)
                    for j in range(2):
                        p.stt(Sb[:, j, :], Sf[:, j, :], col(T.egl, j), J(pS, j), M, A)
                    for j in range(2):
                        p.stt(Sf[:, j, :], Sf[:, j, :], col(T.egl, j), J(pS, j), M, A)
                    p.copy("act", o_t[:, :, sl], f2(po[:, 0:256]))
                    yield

            def gating(tt=tt, o_t=o_t, zs=zs):
                t0g_ = tt * NT
                sq2 = sqr.next()
                p.activation(sq2, o_t, AF.Square)
                ob = obr.next()
                for j in range(2):
                    pn = self.ps()
                    p.matmul(pn, self.ones_f, sq2[:, j, :])
                    rs = rsr.next()
                    p.activation(rs, pn, AF.Ln, scale=1.0 / 128, bias=self.eps_c)
                    p.activation(rs, rs, AF.Exp, scale=-0.5)
                    p.stt(o_t[:, j, :], o_t[:, j, :], og[:, 0:1], rs, M, M)
                    p.tensor_tensor("pool", ob[:, j, :], o_t[:, j, :], zs[:, j, :], M)
                p.dma(p.sp, View(oT_d, oT_d.ap[tt][:, 2 * g:2 * g + 2, :]), ob)

            gens = [stageA(ch) for ch in range(4)]
            if pending is not None:
                gens.append(pending[0]())
            if tt + 1 < ntt:
                gens.append(front1(tt + 1))
            while gens:
                for gnr in list(gens):
                    try:
                        next(gnr)
                    except StopIteration:
                        gens.remove(gnr)
            if pending is not None:
                pending[1]()
            pending = (stageB, gating)
            if tt == ntt - 1:
                for _ in pending[0]():
                    pass
                pending[1]()
                pending = None
    self.pop()


KB.gb_phase = _gb_phase
KB.gdn_phase = _gdn_phase


def _kv_phase(self, hkT_d, wk_d, wv_d, Kd, Vd):
    p = self.p
    self.push()
    wk = p.sbuf([128, 16, 1024], BF16, "wk")
    wv = p.sbuf([128, 16, 1024], BF16, "wv")
    for (w, wd) in ((wk, wk_d), (wv, wv_d)):
        v = wd.ap.rearrange("(c p) n -> p c n", p=128)
        for q4 in range(8):
            p.dma(p.pool, w[:, q4 * 2:(q4 + 1) * 2, :], View(wd, v[:, q4 * 2:(q4 + 1) * 2, :]))
    hr = Rot(p, "kh", [128, 16, NT], BF16, 2)
    ksr = Rot(p, "kst", [128, 8, NT], BF16, 2)
    vsr = Rot(p, "vst", [128, 4, 1024], BF16, 2)
    kv = Kd.ap.rearrange("i e t -> e i t")
    vv = Vd.ap.rearrange("(n p) f -> p n f", p=128)
    for tt in range(NTT):
        t0 = tt * NT
        h_t = hr.next()
        for q4 in range(2):
            p.dma(self.ldq(), h_t[:, q4 * 8:(q4 + 1) * 8, :], View(hkT_d, hkT_d.ap[tt][:, q4 * 8:(q4 + 1) * 8, :]))
        kst = ksr.next()
        for i in range(8):
            ps = self.ps()
            for c in range(16):
                p.matmul(ps, wk[:, c, i * 128:(i + 1) * 128], h_t[:, c, :], start=(c == 0), stop=(c == 15))
            p.copy("act" if i % 2 else "dve", kst[:, i, :], ps)
        p.dma(p.sp, View(Kd, kv[:, :, t0:t0 + NT]), kst)
        vst = vsr.next()
        for ch in range(4):
            for half in range(2):
                ps = self.ps()
                for c in range(16):
                    p.matmul(ps, h_t[:, c, ch * 128:(ch + 1) * 128], wv[:, c, half * 512:(half + 1) * 512],
                             start=(c == 0), stop=(c == 15))
                p.copy("act" if half else "dve", vst[:, ch, half * 512:(half + 1) * 512], ps)
        p.dma(p.sp, View(Vd, vv[:, tt * 4:(tt + 1) * 4, :]), vst)
    self.pop()


DILS = (1, 4, 16)


def _attn_phase(self, hT_d, wq_d, Kd, Vd, bias_d, mask_d, oT_d, Od, nheads=8):
    p = self.p
    self.push()
    M, A = ALU.mult, ALU.add
    biasm = p.sbuf([128, 3, 256], F32, "biasm")
    bias_v = bias_d.ap.rearrange("p (g i b) -> p g i b", g=3, b=256)
    msk = p.sbuf([128, 256], F32, "msk")
    p.dma(p.sp, msk, mask_d)
    wr = Rot(p, "aw", [128, 16, 512], BF16, 2)
    hr = Rot(p, "ah", [128, 16, NT], BF16, 2)
    knat = p.sbuf([128, S], BF16, "knat")
    kp4 = p.sbuf([128, S], BF16, "kp4")
    kp16 = p.sbuf([128, S], BF16, "kp16")
    kps = (knat, kp4, kp16)
    vps = [p.sbuf([128, 32, 128], BF16, f"vp{g}") for g in range(3)]
    qps = [p.sbuf([128, S], BF16, f"qp{g}") for g in range(3)]
    zsd = [p.sbuf([128, S], F32, f"azs{pp_}") for pp_ in range(2)]
    ssp = [[p.sbuf([128, 256], F32, f"as{pp_}{b_}") for b_ in range(4)] for pp_ in range(2)]
    pbp = [[p.sbuf([128, 256], BF16, f"apb{pp_}{b_}") for b_ in range(4)] for pp_ in range(2)]
    ptp_ = [p.sbuf([128, 4, 2, 128], BF16, f"apt{pp_}") for pp_ in range(2)]
    oep = [p.sbuf([128, 4, 132], F32, f"aoe{pp_}") for pp_ in range(2)]
    nmp = [p.sbuf([128, 4], F32, f"anm{pp_}") for pp_ in range(2)]
    ldr = Rot(p, "ald", [128, 3, 132], F32, 3)
    smr = Rot(p, "asm", [128, 16], F32, 3)
    acr = Rot(p, "aac", [128, 128], F32, 3)
    obr = Rot(p, "aob", [128, NT], BF16, 2)
    QS = float(128.0 ** -0.5)

    wts = {}

    def load_w(i_):
        w_ = wr.next()
        wv = wq_d.ap[i_].rearrange("(c p) n -> p c n", p=128)
        for q4 in range(4):
            p.dma(p.pool, w_[:, q4 * 4:(q4 + 1) * 4, :], View(wq_d, wv[:, q4 * 4:(q4 + 1) * 4, :]))
        wts[i_] = w_

    def load_h(tt_):
        h_t = hr.next()
        for q4 in range(2):
            p.dma(self.ldq(), h_t[:, q4 * 8:(q4 + 1) * 8, :], View(hT_d, hT_d.ap[tt_][:, q4 * 8:(q4 + 1) * 8, :]))
        return h_t

    def proj(i_):
        w = wts.pop(i_)
        zs_ = zsd[i_ % 2]
        nxt = load_h(0)
        for tt in range(NTT):
            t0 = tt * NT
            h_t = nxt
            if tt + 1 < NTT:
                nxt = load_h(tt + 1)
            for blk in range(4):
                ps = self.ps()
                for c in range(16):
                    p.matmul(ps, w[:, c, blk * 128:(blk + 1) * 128], h_t[:, c, :], start=(c == 0), stop=(c == 15))
                if blk == 3:
                    p.activation(zs_[:, t0:t0 + NT], ps, AF.Silu)
                else:
                    d = DILS[blk]
                    if d == 1:
                        p.activation(qps[blk][:, t0:t0 + NT], ps, AF.Copy, scale=QS)
                    else:
                        w__ = NT // d
                        dst = qps[blk][:, :].rearrange("p (r u) -> p r u", r=d)[:, :, w__ * tt:w__ * (tt + 1)]
                        src = ps[:, :].rearrange("p (a r) -> p r a", r=d)
                        p.activation(dst, src, AF.Copy, scale=QS)
                yield

    load_w(0)
    for _ in proj(0):
        pass
    for i in range(nheads):
        if i + 1 < nheads:
            load_w(i + 1)
        zs = zsd[i % 2]
        for q4 in range(4):
            p.dma(self.ldq(), knat[:, q4 * 1024:(q4 + 1) * 1024], View(Kd, Kd.ap[i][:, q4 * 1024:(q4 + 1) * 1024]))
        for g, d in ((1, 4), (2, 16)):
            p.copy("pool", kps[g][:, :].rearrange("p (r u) -> p r u", r=d),
                   knat[:, :].rearrange("p (u r) -> p r u", r=d))
        vcol = Vd.ap[:, i * 128:(i + 1) * 128]
        for g, d in enumerate(DILS):
            nb = 32 // d
            if d == 1:
                src = vcol.rearrange("(ub a) f -> a ub f", a=128)
                for q4 in range(4):
                    p.dma(self.ldq(), vps[g][:, q4 * 8:(q4 + 1) * 8, :], View(Vd, src[:, q4 * 8:(q4 + 1) * 8, :]))
            else:
                src = vcol.rearrange("(ub a r) f -> a r ub f", a=128, r=d)
                dst = vps[g][:, :, :].rearrange("a (r ub) f -> a r ub f", r=d)
                for r in range(d):
                    p.dma(self.ldq(), dst[:, r], View(Vd, src[:, r]))
        p.dma(p.sp, biasm, View(bias_d, bias_v[:, :, i, :]))
        for g_ in range(3):
            p.tensor_tensor("pool", biasm[:, g_, :], biasm[:, g_, :], msk, A)

        def grp(g, j0, par):
            d = DILS[g]
            nb = 32 // d
            qp, kp, vp = qps[g], kps[g], vps[g]
            bm = biasm[:, g, :]
            odv = Od.ap[g, i]
            if d == 1:
                odv = odv.rearrange("(ub a) c -> a ub c", a=128)
            else:
                odv = odv.rearrange("(ub a r) c -> a r ub c", a=128, r=d)
            banks = self.psb[4 * par:4 * par + 4]
            blks = list(range(j0, j0 + 4))
            los = [128 if (j % nb == 0) else 0 for j in blks]
            pss = []
            for b, j in enumerate(blks):
                ps_s = banks[b // 2][:, (b % 2) * 256:(b % 2) * 256 + 256]
                lo = los[b]
                p.matmul(ps_s[:, lo:256], qp[:, j * 128:(j + 1) * 128], kp[:, j * 128 - (128 - lo):(j + 1) * 128])
                pss.append(ps_s)
            yield
            oe = oep[par]
            nm = nmp[par]
            ss = ssp[par]
            for b, j in enumerate(blks):
                lo = los[b]
                p.tensor_tensor("dve", ss[b][:, lo:256], pss[b][:, lo:256], bm[:, lo:256], A)
                p.reduce("dve", oe[:, b, 128:129], ss[b][:, lo:256], ALU.max)
            p.tensor_scalar("dve", nm, oe[:, :, 128], -1.0, None, op0=M)
            yield
            pbs = pbp[par]
            for b, j in enumerate(blks):
                lo = los[b]
                p.activation(pbs[b][:, lo:256], ss[b][:, lo:256], AF.Exp, bias=nm[:, b:b + 1],
                             accum_out=oe[:, b, 129:130])
            yield
            ptp = banks[2].bitcast(BF16)
            for b, j in enumerate(blks):
                for kb in range(los[b] // 128, 2):
                    c0 = (b * 2 + kb) * 128
                    p.transpose(ptp[:, c0:c0 + 128], pbs[b][:, kb * 128:(kb + 1) * 128], self.ident_b)
            pt = ptp_[par]
            if all(lo == 0 for lo in los):
                p.copy("dve", pt, ptp[:, 0:1024].rearrange("p (a b c) -> p a b c", a=4, b=2))
            else:
                for b in range(4):
                    kb0 = los[b] // 128
                    p.copy("dve", pt[:, b, kb0:2, :],
                           ptp[:, (b * 2 + kb0) * 128:(b * 2 + 2) * 128].rearrange("p (b c) -> p b c", c=128))
            yield
            po = banks[3]
            for b, j in enumerate(blks):
                kb0 = los[b] // 128
                for kb in range(kb0, 2):
                    p.matmul(po[:, b * 128:(b + 1) * 128], pt[:, b, kb, :], vp[:, j - 1 + kb, :],
                             start=(kb == kb0), stop=(kb == 1))
            p.copy("act", oe[:, :, 0:128], po[:, :].rearrange("p (a b) -> p a b", b=128))
            for b, j in enumerate(blks):
                r, ub = j // nb, j % nb
                dstv = odv[:, ub, 0:130] if d == 1 else odv[:, r, ub, 0:130]
                p.dma(p.sp, View(Od, dstv), oe[:, b, 0:130])
            yield

        todo = [(g, j0) for g in range(3) for j0 in range(0, 32, 4)]
        active = []
        kcnt = 0
        first = True
        while todo or active:
            while len(active) < 2 and todo:
                g_, j0_ = todo.pop(0)
                gn = grp(g_, j0_, kcnt % 2)
                kcnt += 1
                active.append(gn)
                if first:
                    first = False
                    next(gn)
                    next(gn)
                    next(gn)
            for gn in list(active):
                try:
                    next(gn)
                except StopIteration:
                    active.remove(gn)
        def combine(i=i, zs=zs):
            for t4 in range(NTT):
                pso = self.ps()
                for bq in range(4):
                    nblk = t4 * 4 + bq
                    ld = ldr.next()
                    src = Od.ap[:, i, nblk * 128:(nblk + 1) * 128, :].rearrange("g a c -> a g c")
                    p.dma(self.ldq(), ld, View(Od, src))
                    sm = smr.next()
                    p.reduce("dve", sm[:, 0:1], ld[:, :, 128], ALU.max)
                    p.tensor_scalar("dve", sm[:, 1:2], sm[:, 0:1], -1.0, None, op0=M)
                    p.activation(sm[:, 2:5], ld[:, :, 128], AF.Exp, bias=sm[:, 1:2])
                    p.tensor_tensor("dve", sm[:, 10:13], sm[:, 2:5], ld[:, :, 129], M)
                    p.reduce("dve", sm[:, 5:6], sm[:, 10:13], ALU.add)
                    p.reciprocal(sm[:, 6:7], sm[:, 5:6])
                    p.tensor_scalar("dve", sm[:, 7:10], sm[:, 2:5], sm[:, 6:7], None, op0=M)
                    acc = acr.next()
                    p.tensor_scalar("dve", acc, ld[:, 0, 0:128], sm[:, 7:8], None, op0=M)
                    p.stt(acc, ld[:, 1, 0:128], sm[:, 8:9], acc, M, A)
                    p.stt(acc, ld[:, 2, 0:128], sm[:, 9:10], acc, M, A)
                    p.transpose(pso[:, bq * 128:(bq + 1) * 128], acc, self.ident_f)
                ob = obr.next()
                p.tensor_tensor("dve", ob, pso, zs[:, t4 * NT:(t4 + 1) * NT], M)
                p.dma(p.sp, View(oT_d, oT_d.ap[t4][:, i, :]), ob)
                yield

        gens = [combine()]
        if i + 1 < nheads:
            gens.append(proj(i + 1))
        while gens:
            for gnr in list(gens):
                try:
                    next(gnr)
                except StopIteration:
                    gens.remove(gnr)
    self.pop()


KB.kv_phase = _kv_phase
KB.attn_phase = _attn_phase


import ml_dtypes as _mld

_BF = _mld.bfloat16
_PROGS = {}


def _consts_np():
    i = np.arange(128)
    ident = np.eye(128, dtype=np.float32)
    ones = np.ones((128, 128), np.float32)
    U = (i[:, None] <= i[None, :]).astype(np.float32)
    SU = (i[None, :] > i[:, None]).astype(np.float32)
    UU = (i[None, :] >= i[:, None]).astype(np.float32)
    return np.concatenate([ident, ones, U, SU, UU], axis=1)


def _consts2_np():
    i = np.arange(128)
    bm = lambda s: (i[:, None] // s == i[None, :] // s)
    ms = [bm(16), bm(32) & ~bm(16), bm(64) & ~bm(32), ~bm(64)]
    return np.concatenate([np.concatenate([m, m], axis=1) for m in ms], axis=1).astype(np.float32)


def _fm(v):
    return np.ascontiguousarray(np.asarray(v, np.float32).reshape(-1, 128).T)


def _t5_bucket_np(dist):
    n = np.maximum(dist, 0)
    max_exact = 16
    large = max_exact + (np.log(np.maximum(n, 1).astype(np.float32) / np.float32(max_exact))
                         / np.float32(np.log(2048 / max_exact)) * np.float32(32 - max_exact)).astype(np.int32)
    large = np.minimum(large, 31)
    return np.where(n < max_exact, n, large)


def _bias_np(rel_bias, hh):
    q = np.arange(128)[:, None]
    k = np.arange(256)[None, :]
    rel = q + 128 - k
    valid = (rel >= 0) & (rel <= 128)
    out = np.zeros((128, 3, 8, 256), np.float32)
    for g, d in enumerate((1, 4, 16)):
        bk = _t5_bucket_np(rel * d)
        for i in range(8):
            out[:, g, i, :] = np.where(valid, rel_bias[bk, g * 16 + hh * 8 + i], 0.0)
    mask = np.where(valid, 0.0, -30000.0).astype(np.float32)
    return out.reshape(128, 3 * 8 * 256), mask


def _prep_gdn(inp, l, hh):
    w_in = inp["w_in_a"][l]
    win = np.empty((8, 2048, 768), np.float32)
    convw = np.empty((128, 8, 4, 4), np.float32)
    cwl = inp["conv_w_a"][l]
    for g in range(8):
        kh = hh * 8 + g
        cols = [(kh * 128), (2048 + kh * 128), (4096 + (2 * kh) * 128), (4096 + (2 * kh + 1) * 128),
                (8192 + (2 * kh) * 128), (8192 + (2 * kh + 1) * 128)]
        for i, c0 in enumerate(cols):
            win[g, :, i * 128:(i + 1) * 128] = w_in[:, c0:c0 + 128]
        for i, c0 in enumerate(cols[:4]):
            convw[:, g, i, :] = cwl[:, c0:c0 + 128].T
    vh = np.arange(hh * 16, hh * 16 + 16)
    wba = np.concatenate([w_in[:, 12288 + vh], w_in[:, 12320 + vh]], axis=1)
    rep = lambda v: np.ascontiguousarray(np.broadcast_to(np.tile(v[vh], 32)[None, :], (128, 512))).astype(np.float32)
    return dict(win=win, wba=np.ascontiguousarray(wba), convw=convw.reshape(128, 128),
                ogain=np.asarray(inp["o_norm_a"][l], np.float32).reshape(128, 1).copy(),
                alog=rep(inp["a_log"][l]), dtb=rep(inp["dt_bias"][l]))


def _prep_attn(inp, j, hh):
    w = inp["w_in_b"][j]
    wq = np.empty((8, 2048, 512), np.float32)
    for i in range(8):
        hd = hh * 8 + i
        for g in range(3):
            wq[i, :, g * 128:(g + 1) * 128] = w[:, g * 2048 + hd * 128: g * 2048 + (hd + 1) * 128]
        wq[i, :, 384:512] = w[:, 6144 + hd * 128: 6144 + (hd + 1) * 128]
    return wq


def _prog(key, fn, *a):
    if key not in _PROGS:
        _PROGS[key] = fn(*a)
    return _PROGS[key]


def _build_fused():
    nc = bass.Bass("TRN2", target_bir_lowering=False)
    kb = KB(nc)
    p = kb.p
    consts = kb.ext_in("consts", [128, 640])
    kb.load_consts(consts)
    consts2 = kb.ext_in("consts2", [128, 1024])
    kb.load_consts2(consts2)
    xT = kb.ext_in("xT", [NTT, 128, 16, NT])
    cT = kb.ext_in("cT", [128, 16])
    wmod = kb.ext_in("wmod", [4, 2048, 6144])
    bmodT = kb.ext_in("bmodT", [4, 128, 48])
    gainT = kb.ext_in("gainT", [4, 128, 16])
    win = kb.ext_in("win", [2, 2, 8, 2048, 768])
    wba = kb.ext_in("wba", [2, 2, 2048, 32])
    convw = kb.ext_in("convw", [2, 2, 128, 128])
    ogain = kb.ext_in("ogain", [2, 128, 1])
    alog = kb.ext_in("alog", [2, 2, 128, 512])
    dtb = kb.ext_in("dtb", [2, 2, 128, 512])
    wouta = kb.ext_in("wouta", [2, 4096, 2048])
    wkvmod = kb.ext_in("wkvmod", [2048, 4096])
    bkvmodT = kb.ext_in("bkvmodT", [128, 32])
    kvgainT = kb.ext_in("kvgainT", [128, 16])
    wk = kb.ext_in("wk", [2, 2048, 1024])
    wv = kb.ext_in("wv", [2, 2048, 1024])
    wq = kb.ext_in("wq", [2, 2, 8, 2048, 512])
    bias = kb.ext_in("bias", [2, 128, 3 * 8 * 256])
    mask = kb.ext_in("mask", [128, 256])
    woutb = kb.ext_in("woutb", [2, 2048, 2048])
    fgT = kb.ext_in("fgT", [128, 16])
    out = kb.ext_out("outT", [NTT, 128, 16, NT])

    hT = p.dram("hT", [NTT, 128, 16, NT], BF16)
    hkT = p.dram("hkT", [NTT, 128, 16, NT], BF16)
    oTa = p.dram("oTa", [NTT, 128, 32, NT], BF16)
    oTb = p.dram("oTb", [NTT, 128, 16, NT], BF16)
    xA = p.dram("xA", [NTT, 128, 16, NT], F32)
    xB = p.dram("xB", [NTT, 128, 16, NT], F32)
    Kd = [p.dram(f"Kd{h}", [8, 128, S], BF16) for h in range(2)]
    Vd = [p.dram(f"Vd{h}", [S, 1024], BF16) for h in range(2)]
    Od = p.dram("Od", [3, 8, S, 132], F32)
    sub = lambda buf, *idx: Buf(buf.ap[idx] if len(idx) > 1 else buf.ap[idx[0]], buf.name + "_s")

    fg = p.sbuf([128, 16], F32, "fg")
    p.dma(p.sp, fg, fgT)
    x_cur = xT
    x_next = [xA, xB, xA, out]
    for l in range(4):
        kb.push()
        mods = p.sbuf([128, 48], F32, "mods")
        gn = p.sbuf([128, 16], F32, "gn")
        Acol = p.sbuf([128, 16], F32, "Acol")
        p.dma(p.sp, gn, sub(gainT, l))
        kb.mods_phase(cT, sub(wmod, l), sub(bmodT, l), 6144, mods)
        p.stt(Acol, mods[:, 16:32], 1.0, gn, ALU.add, ALU.mult)
        if l < 2:
            for hh in range(2):
                kb.push()
                ba_sb = p.sbuf([128, 2, 512], F32, "ba")
                kb.norm_phase(x_cur, hT, Acol, mods[:, 0:16], sub(wba, l, hh), ba_sb)
                T = kb.gb_phase(ba_sb, sub(alog, l, hh), sub(dtb, l, hh))
                kb.gdn_phase(hT, sub(win, l, hh), sub(convw, l, hh), sub(ogain, l),
                             Buf(oTa.ap[:, :, 16 * hh:16 * (hh + 1), :], "oTa_h"), T)
                kb.pop()
            kb.outproj_phase(oTa, 32, sub(wouta, l), x_cur, x_next[l], mods[:, 32:48])
        else:
            j = l - 2
            if j == 0:
                kb.push()
                kvm = p.sbuf([128, 32], F32, "kvm")
                kg = p.sbuf([128, 16], F32, "kvg")
                Akv = p.sbuf([128, 16], F32, "Akv")
                p.dma(p.sp, kg, kvgainT)
                kb.mods_phase(cT, wkvmod, bkvmodT, 4096, kvm)
                p.stt(Akv, kvm[:, 16:32], 1.0, kg, ALU.add, ALU.mult)
                kb.norm_phase(x_cur, hkT, Akv, kvm[:, 0:16])
                for hh in range(2):
                    kb.kv_phase(hkT, sub(wk, hh), sub(wv, hh), Kd[hh], Vd[hh])
                kb.pop()
            kb.norm_phase(x_cur, hT, Acol, mods[:, 0:16])
            for hh in range(2):
                kb.attn_phase(hT, sub(wq, j, hh), Kd[hh], Vd[hh], sub(bias, hh), mask,
                              Buf(oTb.ap[:, :, 8 * hh:8 * (hh + 1), :], "oTb_h"), Od)
            kb.outproj_phase(oTb, 16, sub(woutb, j), x_cur, x_next[l], mods[:, 32:48],
                             final_gain=(fg if l == 3 else None))
        kb.pop()
        x_cur = x_next[l]
    p.finish()
    p.close()
    return nc


def kernel(**inputs):
    inp = {k: np.asarray(v) for k, v in inputs.items()}
    x = inp["x"].astype(np.float32)
    c = inp["c"].astype(np.float32)
    f32 = lambda a: np.ascontiguousarray(a, dtype=np.float32)
    gd = [[_prep_gdn(inp, l, hh) for hh in range(2)] for l in range(2)]
    stack = lambda key: f32(np.stack([np.stack([gd[l][hh][key] for hh in range(2)]) for l in range(2)]))
    biases = [_bias_np(inp["rel_bias"].astype(np.float32), hh) for hh in range(2)]
    wkv = inp["w_kv"].astype(np.float32)
    shared = {
        "consts": _consts_np(), "consts2": _consts2_np(),
        "wmod": f32(inp["w_mod"]),
        "bmodT": f32(np.stack([_fm(inp["b_mod"][l]) for l in range(4)])),
        "gainT": f32(np.stack([_fm(inp["norm_gain"][l]) for l in range(4)])),
        "win": stack("win"), "wba": stack("wba"), "convw": stack("convw"),
        "ogain": f32(np.stack([gd[l][0]["ogain"] for l in range(2)])),
        "alog": stack("alog"), "dtb": stack("dtb"),
        "wouta": f32(inp["w_out_a"]),
        "wkvmod": f32(inp["w_kv_mod"]), "bkvmodT": _fm(inp["b_kv_mod"]), "kvgainT": _fm(inp["kv_gain"]),
        "wk": f32(np.stack([wkv[:, hh * 1024:(hh + 1) * 1024] for hh in range(2)])),
        "wv": f32(np.stack([wkv[:, 2048 + hh * 1024:2048 + (hh + 1) * 1024] for hh in range(2)])),
        "wq": f32(np.stack([np.stack([_prep_attn(inp, j, hh) for hh in range(2)]) for j in range(2)])),
        "bias": f32(np.stack([biases[hh][0] for hh in range(2)])), "mask": biases[0][1],
        "woutb": f32(inp["w_out_b"]), "fgT": _fm(inp["final_gain"]),
    }
    nc = _prog("F", _build_fused)
    in_maps = []
    for core in range(8):
        b = core // 2
        m = dict(shared)
        m["xT"] = np.ascontiguousarray(x[b].reshape(NTT, NT, 16, 128).transpose(0, 3, 2, 1))
        m["cT"] = _fm(c[b])
        in_maps.append(m)
    res = run_bass_kernel_spmd(nc, in_maps, core_ids=list(range(8))).results
    out = np.empty((4, S, D), np.float32)
    for b in range(4):
        out[b] = np.asarray(res[2 * b]["outT"]).transpose(0, 3, 2, 1).reshape(S, D)
    return out
```

```python
import numpy as np
import concourse.bass as bass
import concourse.mybir as mybir
from concourse.bass_utils import run_bass_kernel_spmd

F32 = mybir.dt.float32
BF16 = mybir.dt.bfloat16
ALU = mybir.AluOpType
AF = mybir.ActivationFunctionType
AX = mybir.AxisListType

SAME_ENG_RAW = True


class Buf:
    def __init__(self, ap, name=""):
        self.ap = ap
        self.name = name
        self.writers = {}
        self.readers = {}
        self.war = {}

    def __getitem__(self, idx):
        return View(self, self.ap[idx])

    def bitcast(self, dt):
        return View(self, self.ap.bitcast(dt))

    def rearrange(self, *a, **k):
        return View(self, self.ap.rearrange(*a, **k))


class View:
    def __init__(self, buf, ap):
        self.buf = buf
        self.ap = ap

    def __getitem__(self, idx):
        return View(self.buf, self.ap[idx])

    def rearrange(self, *a, **k):
        return View(self.buf, self.ap.rearrange(*a, **k))

    def bitcast(self, dt):
        return View(self.buf, self.ap.bitcast(dt))


def _ap(x):
    if isinstance(x, View):
        return x.ap
    if isinstance(x, Buf):
        return x.ap
    return x


def _bufs(xs):
    out = []
    for x in xs:
        if isinstance(x, View):
            out.append(x.buf)
        elif isinstance(x, Buf):
            out.append(x)
    return out


class Eng:
    def __init__(self, name, h, sem, si):
        self.name = name
        self.h = h
        self.sem = sem
        self.si = si
        self.cnt = 0
        self.waited = {}
        self.dma_i = 0
        self.dma_sems = []


class Prog:
    def __init__(self, nc, n_dma_sems=(40, 16, 24)):
        self.nc = nc
        self.sems = []
        self.engs = {}
        for name, h in [("pe", nc.tensor), ("dve", nc.vector), ("act", nc.scalar),
                        ("pool", nc.gpsimd), ("sp", nc.sync)]:
            s = nc.alloc_semaphore(name=f"sem_{name}")
            self.sems.append(s)
            self.engs[name] = Eng(name, h, s, len(self.sems) - 1)
        for qn, n in zip(("sp", "act", "pool"), n_dma_sems):
            e = self.engs[qn]
            for i in range(n):
                s = nc.alloc_semaphore(name=f"dsem_{qn}{i}")
                self.sems.append(s)
                e.dma_sems.append(len(self.sems) - 1)
        self.pe, self.dve, self.act, self.pool, self.sp = (
            self.engs[k] for k in ("pe", "dve", "act", "pool", "sp"))
        self._ctx = []
        self.n_wait = 0
        self.n_ins = 0

    def sbuf(self, shape, dt, name):
        self._uid = getattr(self, "_uid", 0) + 1
        name = f"{name}_u{self._uid}"
        cm = self.nc.sbuf_tensor(name, list(shape), dt)
        t = cm.__enter__()
        self._ctx.append(cm)
        return Buf(t.ap() if hasattr(t, "ap") else t[:], name)

    def psum(self, shape, dt, name):
        cm = self.nc.psum_tensor(name, list(shape), dt)
        t = cm.__enter__()
        self._ctx.append(cm)
        b = Buf(t.ap() if hasattr(t, "ap") else t[:], name)
        b.excl = True
        return b

    def dram(self, name, shape, dt, kind="Internal"):
        t = self.nc.dram_tensor(name, list(shape), dt, kind=kind)
        return Buf(t.ap(), name)

    def close(self):
        for cm in reversed(self._ctx):
            cm.__exit__(None, None, None)
        self._ctx = []

    def _wait(self, eng, si, val):
        if eng.waited.get(si, 0) >= val:
            return
        eng.h.wait_ge(self.sems[si], val)
        eng.waited[si] = val
        self.n_wait += 1

    def _sync(self, eng, reads, writes):
        deps = {}
        for b in reads:
            for si, v in b.writers.items():
                if si == eng.si and (eng.name == "pe" or not SAME_ENG_RAW):
                    continue
                if deps.get(si, 0) < v:
                    deps[si] = v
            if getattr(b, "excl", False):
                for si, v in b.readers.items():
                    if si == eng.si:
                        continue
                    if deps.get(si, 0) < v:
                        deps[si] = v
        for b in writes:
            for d in (b.writers, b.readers, b.war):
                for si, v in d.items():
                    if si == eng.si:
                        continue
                    if deps.get(si, 0) < v:
                        deps[si] = v
        for si, v in deps.items():
            self._wait(eng, si, v)

    def _mark(self, tok, reads, writes):
        si, v = tok
        for b in writes:
            if b.readers:
                b.war = b.readers
                b.readers = {}
                b.writers = {}
            b.writers[si] = max(b.writers.get(si, 0), v)
        wset = set(id(b) for b in writes)
        for b in reads:
            if id(b) in wset:
                continue
            b.readers[si] = max(b.readers.get(si, 0), v)

    def op(self, eng, fn, *, outs, ins, **kw):
        rb, wb = _bufs(ins), _bufs(outs)
        self._sync(eng, rb, wb)
        inst = fn(**kw)
        eng.cnt += 1
        inst.then_inc(eng.sem, 1)
        self._mark((eng.si, eng.cnt), rb, wb)
        self.n_ins += 1
        return inst

    def dma(self, q, out, in_, **kw):
        rb, wb = _bufs([in_]), _bufs([out])
        self._sync_dma(q, rb, wb)
        k = len(q.dma_sems)
        si = q.dma_sems[q.dma_i % k]
        val = 16 * (q.dma_i // k + 1)
        if val > 16:
            self._wait(q, si, val - 16)
        q.dma_i += 1
        inst = q.h.dma_start(out=_ap(out), in_=_ap(in_), **kw)
        inst.then_inc(self.sems[si], 16)
        self._mark((si, val), rb, wb)
        self.n_ins += 1
        return (si, val)

    def _sync_dma(self, eng, reads, writes):
        deps = {}
        dma_sis = set()
        for e in self.engs.values():
            dma_sis.update(e.dma_sems)
        for b in reads:
            for si, v in b.writers.items():
                if deps.get(si, 0) < v:
                    deps[si] = v
        for b in writes:
            for si, v in b.readers.items():
                if deps.get(si, 0) < v:
                    deps[si] = v
            for si, v in b.war.items():
                if deps.get(si, 0) < v:
                    deps[si] = v
            for si, v in b.writers.items():
                if si in dma_sis:
                    continue
                if deps.get(si, 0) < v:
                    deps[si] = v
        for si, v in deps.items():
            self._wait(eng, si, v)

    def wait_all(self, eng, bufs):
        for b in bufs:
            for d in (b.writers, b.readers):
                for si, v in d.items():
                    self._wait(eng, si, v)

    def matmul(self, out, lhsT, rhs, start=True, stop=True, **kw):
        return self.op(self.pe, self.nc.tensor.matmul, outs=[out], ins=[lhsT, rhs],
                       out=_ap(out), lhsT=_ap(lhsT), rhs=_ap(rhs), start=start, stop=stop, **kw)

    def transpose(self, out, in_, ident):
        return self.op(self.pe, self.nc.tensor.transpose, outs=[out], ins=[in_, ident],
                       out=_ap(out), in_=_ap(in_), identity=_ap(ident))

    def _e(self, eng):
        return eng if isinstance(eng, Eng) else self.engs[eng]

    def activation(self, out, in_, func, bias=None, scale=None, accum_out=None, eng="act"):
        e = self._e(eng)
        ins = [in_]
        kw = dict(out=_ap(out), in_=_ap(in_), func=func)
        if bias is not None:
            kw["bias"] = _ap(bias)
            ins.append(bias)
        if scale is not None:
            kw["scale"] = _ap(scale)
            ins.append(scale)
        outs = [out]
        if accum_out is not None:
            kw["accum_out"] = _ap(accum_out)
            outs.append(accum_out)
        return self.op(e, e.h.activation, outs=outs, ins=ins, **kw)

    def tensor_scalar(self, eng, out, in0, s1, s2=None, op0=ALU.mult, op1=None, accum_out=None):
        e = self._e(eng)
        ins = [in0, s1, s2]
        kw = dict(out=_ap(out), in0=_ap(in0), scalar1=_ap(s1), scalar2=_ap(s2), op0=op0)
        if op1 is not None:
            kw["op1"] = op1
        outs = [out]
        if accum_out is not None:
            kw["accum_out"] = _ap(accum_out)
            outs.append(accum_out)
        return self.op(e, e.h.tensor_scalar, outs=outs, ins=ins, **kw)

    def tensor_tensor(self, eng, out, in0, in1, op):
        e = self._e(eng)
        return self.op(e, e.h.tensor_tensor, outs=[out], ins=[in0, in1],
                       out=_ap(out), in0=_ap(in0), in1=_ap(in1), op=op)

    def stt(self, out, in0, scalar, in1, op0, op1, eng="dve", accum_out=None):
        e = self._e(eng)
        kw = dict(out=_ap(out), in0=_ap(in0), scalar=_ap(scalar), in1=_ap(in1), op0=op0, op1=op1)
        outs = [out]
        if accum_out is not None:
            kw["accum_out"] = _ap(accum_out)
            outs.append(accum_out)
        return self.op(e, e.h.scalar_tensor_tensor, outs=outs, ins=[in0, scalar, in1], **kw)

    def copy(self, eng, out, in_):
        e = self._e(eng)
        if e.name == "act":
            return self.op(e, e.h.activation, outs=[out], ins=[in_], out=_ap(out), in_=_ap(in_),
                           func=AF.Copy)
        return self.op(e, e.h.tensor_copy, outs=[out], ins=[in_], out=_ap(out), in_=_ap(in_))

    def memset(self, eng, out, val):
        e = self._e(eng)
        return self.op(e, e.h.memset, outs=[out], ins=[], ap=_ap(out), constant=val)

    def reduce(self, eng, out, in_, op, axis=AX.X):
        e = self._e(eng)
        return self.op(e, e.h.tensor_reduce, outs=[out], ins=[in_], out=_ap(out), in_=_ap(in_),
                       axis=axis, op=op)

    def reciprocal(self, out, in_):
        e = self.dve
        return self.op(e, e.h.reciprocal, outs=[out], ins=[in_], out=_ap(out), in_=_ap(in_))

    def barrier(self):
        self.finish()
        inst = self.sp.h.nop()
        self.sp.cnt += 1
        inst.then_inc(self.sp.sem, 1)
        for e in self.engs.values():
            if e is not self.sp:
                self._wait(e, self.sp.si, self.sp.cnt)

    def finish(self):
        for e in self.engs.values():
            if e.cnt:
                self._wait(self.sp, e.si, e.cnt)
            k = len(e.dma_sems)
            for j in range(min(k, e.dma_i)):
                n_uses = (e.dma_i - 1 - j) // k + 1
                self._wait(self.sp, e.dma_sems[j], 16 * n_uses)


import os
STOP = int(os.environ.get('GDN_STOP', '99'))

D = 2048
S = 4096
NT = 512
NTT = S // NT
NCH = 16
EPS = 1e-6


class Rot:
    def __init__(self, p, name, shape, dt, n=2):
        self.bufs = [p.sbuf(shape, dt, f"{name}{i}") for i in range(n)]
        self.i = 0

    def next(self):
        b = self.bufs[self.i % len(self.bufs)]
        self.i += 1
        return b


class KB:
    def __init__(self, nc):
        self.nc = nc
        self.p = Prog(nc)
        p = self.p
        self.psb = [p.psum([128, 512], F32, f"ps{i}") for i in range(8)]
        self.psi = 0
        self.marks = []
        self.din = {}
        self.dq = 0

    def ps(self):
        b = self.psb[self.psi % 8]
        self.psi += 1
        return b

    def ext_in(self, name, shape, dt=F32):
        t = self.nc.dram_tensor(name, list(shape), dt, kind="ExternalInput")
        b = Buf(t.ap(), name)
        self.din[name] = b
        return b

    def ext_out(self, name, shape, dt=F32):
        t = self.nc.dram_tensor(name, list(shape), dt, kind="ExternalOutput")
        return Buf(t.ap(), name)

    def push(self):
        self.marks.append(len(self.p._ctx))

    def pop(self):
        m = self.marks.pop()
        p = self.p
        p.barrier()
        while len(p._ctx) > m:
            cm = p._ctx.pop()
            cm.__exit__(None, None, None)

    def dump(self, idx, view, w=128):
        d = getattr(self, "dbg_d", None)
        if d is None:
            return
        self.p.dma(self.p.pool, d[:, idx, 0:w], view)

    def ldq(self):
        return self.p.sp

    def load_consts2(self, consts2_d):
        p = self.p
        self.bmk = p.sbuf([128, 4, 256], BF16, "bmk")
        self.push()
        cf2 = p.sbuf([128, 1024], F32, "cf2")
        p.dma(p.sp, cf2, consts2_d)
        p.copy("dve", self.bmk, cf2[:, :].rearrange("p (a b) -> p a b", b=256))
        self.pop()

    def load_consts(self, consts_d):
        p = self.p
        self.cf = p.sbuf([128, 5 * 128], F32, "cf")
        p.dma(p.sp, self.cf, consts_d)
        self.ident_f = self.cf[:, 0:128]
        self.ones_f = self.cf[:, 128:256]
        self.U_f = self.cf[:, 256:384]
        self.masks = self.cf[:, 384:640]
        self.cb = p.sbuf([128, 256], BF16, "cb")
        p.copy("dve", self.cb, self.cf[:, 0:256])
        self.ident_b = self.cb[:, 0:128]
        self.ones_b = self.cb[:, 128:256]
        self.cc = p.sbuf([128, 4], F32, "cc")
        p.memset("dve", self.cc[:, 0:1], EPS)
        p.memset("dve", self.cc[:, 1:2], 1.0)
        p.memset("dve", self.cc[:, 2:3], 0.0)
        p.memset("dve", self.cc[:, 3:4], float(np.log(128.0 ** -0.5)))
        self.eps_c = self.cc[:, 0:1]
        self.one_c = self.cc[:, 1:2]
        self.zero_c = self.cc[:, 2:3]
        self.lnq_c = self.cc[:, 3:4]

    def mods_phase(self, cT_d, wmod_d, bmodT_d, ncols, out_sb):
        p = self.p
        self.push()
        nj = ncols // 128
        cT = p.sbuf([128, 16], F32, "cT")
        cact = p.sbuf([128, 16], F32, "cact")
        bm = p.sbuf([128, nj], F32, "bm")
        p.dma(p.sp, cT, cT_d)
        p.dma(p.sp, bm, bmodT_d)
        p.activation(cact, cT, AF.Silu)
        wr = Rot(p, "wm", [128, 16, 512], F32, 2)
        wv = wmod_d.ap.rearrange("(c p) n -> p c n", p=128)
        ps = self.ps()
        for blk in range(ncols // 512):
            w = wr.next()
            for q4 in range(4):
                p.dma(self.ldq(), w[:, q4 * 4:(q4 + 1) * 4, :],
                      View(wmod_d, wv[:, q4 * 4:(q4 + 1) * 4, blk * 512:(blk + 1) * 512]))
            for jj in range(4):
                j = blk * 4 + jj
                for c in range(16):
                    p.matmul(ps[:, j:j + 1], w[:, c, jj * 128:(jj + 1) * 128], cact[:, c:c + 1],
                             start=(c == 0), stop=(c == 15))
        p.tensor_tensor("dve", out_sb, ps[:, 0:nj], bm, ALU.add)
        self.pop()

    def norm_phase(self, xT_d, hT_d, Acol, Scol, wba_d=None, ba_sb=None):
        p = self.p
        self.push()
        xr = Rot(p, "nx", [128, 16, NT], F32, 2)
        hr = Rot(p, "nh", [128, 16, NT], BF16, 2)
        sr = Rot(p, "nsq", [128, NT], F32, 3)
        tr = Rot(p, "ntm", [128, NT], F32, 3)
        rr = Rot(p, "nrs", [128, NT], F32, 2)
        wba_list = []
        if wba_d is not None:
            wds = wba_d if isinstance(wba_d, (list, tuple)) else [wba_d]
            bas = ba_sb if isinstance(ba_sb, (list, tuple)) else [ba_sb]
            for wd_ in wds:
                wba = p.sbuf([128, 16, 32], BF16, "wba")
                p.dma(p.pool, wba, View(wd_, wd_.ap.rearrange("(c p) n -> p c n", p=128)))
                wba_list.append(wba)
        for tt in range(NTT):
            t0 = tt * NT
            x_t = xr.next()
            for q4 in range(4):
                p.dma(self.ldq(), x_t[:, q4 * 4:(q4 + 1) * 4, :],
                      View(xT_d, xT_d.ap[tt][:, q4 * 4:(q4 + 1) * 4, :]))
            ps = self.ps()
            for c in range(16):
                s = sr.next()
                p.tensor_tensor("pool", s, x_t[:, c, :], x_t[:, c, :], ALU.mult)
                p.matmul(ps, self.ones_f, s, start=(c == 0), stop=(c == 15))
            rstd = rr.next()
            p.activation(rstd, ps, AF.Ln, scale=1.0 / D, bias=self.eps_c)
            p.activation(rstd, rstd, AF.Exp, scale=-0.5)
            h_t = hr.next()
            for c in range(16):
                tmp = tr.next()
                p.tensor_tensor("dve", tmp, x_t[:, c, :], rstd, ALU.mult)
                p.activation(h_t[:, c, :], tmp, AF.Identity, scale=Acol[:, c:c + 1], bias=Scol[:, c:c + 1])
            for q4 in range(2):
                p.dma(p.sp, View(hT_d, hT_d.ap[tt][:, q4 * 8:(q4 + 1) * 8, :]),
                      h_t[:, q4 * 8:(q4 + 1) * 8, :])
            for wba, ba_o in zip(wba_list, bas if wba_list else []):
                pb = self.ps()
                for ch in range(4):
                    for half in range(2):
                        for c in range(16):
                            p.matmul(pb[:, half * 64 + ch * 16:half * 64 + (ch + 1) * 16],
                                     h_t[:, c, ch * 128:(ch + 1) * 128],
                                     wba[:, c, half * 16:(half + 1) * 16], start=(c == 0), stop=(c == 15))
                for half in range(2):
                    p.copy("dve", ba_o[:, half, tt * 64:(tt + 1) * 64], pb[:, half * 64:(half + 1) * 64])
        self.pop()

    def outproj_phase(self, oT_d, nk, wout_d, xT_d, xTo_d, gate_col, t_lo=0, t_hi=NTT,
                      final_gain=None):
        p = self.p
        nh = 2 if nk == 32 else 1
        CW = D // nh
        NCc = 16 // nh
        assert final_gain is None or nh == 1
        wv = wout_d.ap.rearrange("(k p) n -> p k n", p=128)
        for half in range(nh):
            self.push()
            w = p.sbuf([128, nk, CW], BF16, "wo")
            for k in range(nk):
                p.dma(p.pool, w[:, k, :], View(wout_d, wv[:, k, half * CW:(half + 1) * CW]))
            orr = Rot(p, "oo", [128, nk, NT], BF16, 2)
            xr = Rot(p, "ox", [128, NCc, NT], F32, 2)
            sr = Rot(p, "osq", [128, NT], F32, 3)
            for tt in range(t_lo, t_hi):
                t0 = tt * NT
                o_t = orr.next()
                nq = max(1, nk // 4)
                for q4 in range(nk // nq):
                    p.dma(self.ldq(), o_t[:, q4 * nq:(q4 + 1) * nq, :],
                          View(oT_d, oT_d.ap[tt][:, q4 * nq:(q4 + 1) * nq, :]))
                x_t = xr.next()
                c0 = half * NCc
                for q4 in range(2):
                    h2 = NCc // 2
                    p.dma(self.ldq(), x_t[:, q4 * h2:(q4 + 1) * h2, :],
                          View(xT_d, xT_d.ap[tt][:, c0 + q4 * h2:c0 + (q4 + 1) * h2, :]))
                for c in range(NCc):
                    ps = self.ps()
                    for k in range(nk):
                        p.matmul(ps, w[:, k, c * 128:(c + 1) * 128], o_t[:, k, :],
                                 start=(k == 0), stop=(k == nk - 1))
                    p.stt(x_t[:, c, :], ps, gate_col[:, c0 + c:c0 + c + 1], x_t[:, c, :], ALU.mult, ALU.add)
                if final_gain is not None:
                    pss = self.ps()
                    for c in range(16):
                        s = sr.next()
                        p.tensor_tensor("pool", s, x_t[:, c, :], x_t[:, c, :], ALU.mult)
                        p.matmul(pss, self.ones_f, s, start=(c == 0), stop=(c == 15))
                    rstd = sr.next()
                    p.activation(rstd, pss, AF.Ln, scale=1.0 / D, bias=self.eps_c)
                    p.activation(rstd, rstd, AF.Exp, scale=-0.5)
                    for c in range(16):
                        p.stt(x_t[:, c, :], x_t[:, c, :], final_gain[:, c:c + 1], rstd, ALU.mult, ALU.mult)
                for q4 in range(2):
                    h2 = NCc // 2
                    p.dma(p.sp, View(xTo_d, xTo_d.ap[tt][:, c0 + q4 * h2:c0 + (q4 + 1) * h2, :]),
                          x_t[:, q4 * h2:(q4 + 1) * h2, :])
            self.pop()


class T_:
    pass


def _gb_phase(self, ba_sb, alog_d, dtb_d):
    p = self.p
    T = T_()
    mk = lambda nm: p.sbuf([128, 512], F32, nm)
    T.beta, T.g, T.gc, T.egc, T.ekd, T.egl = (mk(n) for n in ("beta", "g", "gc", "egc", "ekd", "egl"))
    self.push()
    al, dt, x, ax, e = (mk(n) for n in ("al", "dt", "gx", "gax", "ge"))
    p.dma(p.sp, al, alog_d)
    p.dma(p.sp, dt, dtb_d)
    bv = ba_sb[:, 0, :]
    av = ba_sb[:, 1, :]
    v3 = lambda t: t
    p.activation(v3(e), bv, AF.Exp, scale=-1.0)
    p.tensor_scalar("dve", e, e, 1.0, None, op0=ALU.add)
    p.reciprocal(T.beta, e)
    GB = int(os.environ.get('GB', '9'))
    if GB <= 1:
        self.pop(); return T
    p.tensor_tensor("dve", v3(x), av, v3(dt), ALU.add)
    p.stt(ax, x, -1.0, x, ALU.mult, ALU.max)
    p.activation(e, ax, AF.Exp, scale=-1.0)
    p.activation(e, e, AF.Ln, bias=self.one_c)
    p.tensor_scalar("dve", ax, x, 0.0, None, op0=ALU.max)
    p.tensor_tensor("dve", x, ax, e, ALU.add)
    if GB <= 2:
        self.pop(); return T
    p.activation(al, al, AF.Exp)
    p.stt(T.g, x, -1.0, al, ALU.mult, ALU.mult)
    if GB <= 3:
        self.pop(); return T
    ps = self.ps()
    p.matmul(ps, self.U_f, T.g)
    p.copy("dve", T.gc, ps)
    if GB <= 4:
        self.pop(); return T
    p.activation(T.egc, ps, AF.Exp)
    if GB <= 5:
        self.pop(); return T
    ps2 = self.ps()
    p.matmul(ps2, self.ones_f, T.g)
    p.activation(T.egl, ps2, AF.Exp)
    if GB <= 6:
        self.pop(); return T
    p.tensor_tensor("dve", x, ps2, T.gc, ALU.subtract)
    p.activation(T.ekd, x, AF.Exp)
    self.pop()
    return T


def _gdn_phase(self, hT_d, win_d, convw_d, ogain_d, oT_d, T, ngroups=8, ntt=NTT):
    p = self.p
    self.push()
    cw = p.sbuf([128, 128], F32, "cw")
    og = p.sbuf([128, 1], F32, "og")
    p.dma(p.sp, cw, convw_d)
    p.dma(p.sp, og, ogain_d)
    ident2 = p.sbuf([128, 2, 128], F32, "ident2")
    p.copy("dve", ident2[:, 0, :], self.ident_f)
    p.copy("dve", ident2[:, 1, :], self.ident_f)
    wr = Rot(p, "gw", [128, 16, 768], BF16, 1)
    hr = Rot(p, "gh", [128, 16, NT], BF16, 2)
    xinr = Rot(p, "gxin", [128, 4, NT + 3], F32, 2)
    cvr = Rot(p, "gcv", [128, 4, NT], F32, 1)
    zsr = Rot(p, "gzs", [128, 2, NT], F32, 3)
    sqr = Rot(p, "gsq", [128, 2, NT], F32, 1)
    rsr = Rot(p, "grs", [128, NT], F32, 2)
    qbr = Rot(p, "gqb", [128, NT], BF16, 1)
    kbr = Rot(p, "gkb", [128, NT], BF16, 1)
    kfr = Rot(p, "gkf", [128, NT], F32, 1)
    otr = Rot(p, "got", [128, 2, NT], F32, 1)
    obr = Rot(p, "gob", [128, 2, NT], BF16, 1)
    HB = [[{nm: p.sbuf([128, 2, 128], BF16, f"hb{pp_}{cc_}{nm}") for nm in ("X", "vt", "nw", "kd", "attn", "qd")}
           for cc_ in range(4)] for pp_ in range(2)]
    RS = []
    for ch_ in range(4):
        R = T_()
        for nm, dt in (("kg", BF16), ("ug", F32), ("dd", F32), ("egr", F32),
                       ("Nb", BF16), ("Y", BF16), ("Nt", BF16), ("Pt", BF16),
                       ("P", BF16), ("vn", BF16), ("Et", BF16)):
            setattr(R, nm, Rot(p, f"c{ch_}" + nm, [128, 2, 128], dt, 3 if nm in ("Y", "Et") else (2 if nm in ("P", "Pt") else 1)))
        R.kkq = Rot(p, f"c{ch_}kkq", [128, 256], F32, 1)
        RS.append(R)
    Sf = p.sbuf([128, 2, 128], F32, "Sf")
    Sb = p.sbuf([128, 2, 128], BF16, "Sb")
    f2 = lambda v: v.rearrange("p (a b) -> p a b", b=128)
    M, A, SUB, MIN = ALU.mult, ALU.add, ALU.subtract, ALU.min

    for g in range(ngroups):
        if STOP < 0:
            break
        w = wr.next()
        wv = win_d.ap[g].rearrange("(c p) n -> p c n", p=128)
        for q4 in range(4):
            p.dma(p.pool, w[:, q4 * 4:(q4 + 1) * 4, :], View(win_d, wv[:, q4 * 4:(q4 + 1) * 4, :]))
        p.memset("pool", Sf, 0.0)
        p.memset("pool", Sb, 0.0)
        pending = None
        hbuf = {}
        F1out = {}
        fstate = {"xin_prev": None}

        def load_h(tt_):
            h_t = hr.next()
            for q4 in range(2):
                p.dma(self.ldq(), h_t[:, q4 * 8:(q4 + 1) * 8, :],
                      View(hT_d, hT_d.ap[tt_][:, q4 * 8:(q4 + 1) * 8, :]))
            hbuf[tt_] = h_t

        def front1(tt_):
            h_t = hbuf.pop(tt_)
            xin = xinr.next()
            zs_ = zsr.next()
            F1out[tt_] = (xin, zs_)
            for blk in range(6):
                ps = self.ps()
                for c in range(16):
                    p.matmul(ps, w[:, c, blk * 128:(blk + 1) * 128], h_t[:, c, :],
                             start=(c == 0), stop=(c == 15))
                if blk < 4:
                    p.copy("act", xin[:, blk, 3:3 + NT], ps)
                else:
                    p.activation(zs_[:, blk - 4, :], ps, AF.Silu)
                yield
            if fstate["xin_prev"] is None:
                p.memset("pool", xin[:, :, 0:3], 0.0)
            else:
                p.copy("pool", xin[:, :, 0:3], fstate["xin_prev"][:, :, NT:NT + 3])
            fstate["xin_prev"] = xin
            yield

        load_h(0)
        if ntt > 1:
            load_h(1)
        for _ in front1(0):
            pass
        for tt in range(ntt):
            t0g = tt * NT
            if tt + 2 < ntt:
                load_h(tt + 2)
            xin, zs = F1out.pop(tt)
            cv = cvr.next()
            for blk in range(4):
                cb = (g * 4 + blk) * 4
                p.tensor_scalar("dve", cv[:, blk, :], xin[:, blk, 0:NT], cw[:, cb:cb + 1], None, op0=M)
                for j in range(1, 4):
                    p.stt(cv[:, blk, :], xin[:, blk, j:j + NT], cw[:, cb + j:cb + j + 1], cv[:, blk, :], M, A)
            p.activation(cv, cv, AF.Silu)
            sq = sqr.next()
            p.activation(sq, cv[:, 0:2, :], AF.Square)
            qb, kb, kf = qbr.next(), kbr.next(), kfr.next()
            for blk in range(2):
                pn = self.ps()
                p.matmul(pn, self.ones_f, sq[:, blk, :])
                rs = rsr.next()
                p.activation(rs, pn, AF.Ln, bias=self.eps_c)
                if blk == 0:
                    p.activation(rs, rs, AF.Exp, scale=-0.5, bias=self.lnq_c)
                    p.tensor_tensor("dve", qb, cv[:, 0, :], rs, M)
                else:
                    p.activation(rs, rs, AF.Exp, scale=-0.5)
                    p.tensor_tensor("dve", kf, cv[:, 1, :], rs, M)
                    p.copy("act", kb, kf)
            o_t = otr.next()
            if STOP <= 2:
                continue
            J = lambda ps_, j: ps_[:, j * 128:(j + 1) * 128]
            f3 = lambda v: v.rearrange("p (a b) -> p a b", b=128)
            mk = lambda i: f3(self.bmk[:, i, :])
            par = tt % 2
            hand = {}

            def stageA(ch, par=par, tt=tt, kf=kf, cv=cv, kb=kb, qb=qb, hand=hand):
                n = tt * 4 + ch
                sl = slice(ch * 128, (ch + 1) * 128)
                cjs = [n * 16 + 2 * g + j for j in (0, 1)]
                col = lambda t, j: t[:, cjs[j]:cjs[j] + 1]
                R = RS[ch]
                pT = self.ps()
                p.transpose(pT[:, 0:128], kf[:, sl], self.ident_f)
                p.transpose(pT[:, 128:256], cv[:, 2, sl], self.ident_f)
                p.transpose(pT[:, 256:384], cv[:, 3, sl], self.ident_f)
                HBc = HB[par][ch]
                kg, kd, vt = R.kg.next(), HBc['kd'], HBc['vt']
                for j in range(2):
                    p.tensor_scalar("dve", kg[:, j, :], pT[:, 0:128], col(T.egc, j), None, op0=M)
                    p.activation(kd[:, j, :], pT[:, 0:128], AF.Identity, scale=col(T.ekd, j))
                p.copy("act", vt, f2(pT[:, 128:384]))
                pk = self.ps()
                p.matmul(pk[:, 0:128], kb[:, sl], kb[:, sl])
                p.matmul(pk[:, 128:256], kb[:, sl], qb[:, sl])
                kkq = R.kkq.next()
                p.tensor_tensor("dve", kkq, pk[:, 0:256], self.masks, M)
                ug = R.ug.next()
                for j in range(2):
                    p.activation(ug[:, j, :], self.U_f, AF.Identity, scale=col(T.g, j))
                yield
                pg = self.ps()
                for j in range(2):
                    p.matmul(J(pg, j), self.ones_f, ug[:, j, :])
                dd, egr = R.dd.next(), R.egr.next()
                for j in range(2):
                    p.tensor_scalar("dve", dd[:, j, :], J(pg, j), col(T.gc, j), 0.0, op0=SUB, op1=MIN)
                p.activation(egr, f2(pg[:, 0:256]), AF.Exp)
                yield
                p.activation(dd, dd, AF.Exp)
                yield
                Nb = R.Nb.next()
                for j in range(2):
                    p.stt(Nb[:, j, :], kkq[:, 0:128], col(T.beta, j), dd[:, j, :], M, M)
                attn, qd = HBc['attn'], HBc['qd']
                for j in range(2):
                    p.tensor_tensor("pool", attn[:, j, :], kkq[:, 128:256], dd[:, j, :], M)
                    p.tensor_tensor("pool", qd[:, j, :], qb[:, sl], egr[:, j, :], M)
                yield
                pt = self.ps().bitcast(BF16)
                for j in range(2):
                    p.transpose(J(pt, j), Nb[:, j, :], self.ident_b)
                Ntb = R.Nt.next()
                p.copy("act", Ntb, f2(pt[:, 0:256]))
                N16 = R.P.next()
                p.tensor_tensor("pool", N16, Nb, mk(0), M)
                Y = R.Y.next()
                p.stt(Y, N16, -1.0, ident2, M, A)
                yield
                N16t = R.Pt.next()
                p.tensor_tensor("pool", N16t, Ntb, mk(0), M)
                Ets = []
                for lv in range(3):
                    et = R.Et.next()
                    p.tensor_tensor("pool", et, Ntb, mk(1 + lv), M)
                    Ets.append(et)
                yield
                P_, Pt_ = N16, N16t

                def lvl_a(k, P_, Pt_):
                    pp = self.ps()
                    for j in range(2):
                        p.matmul(J(pp, j), P_[:, j, :], Pt_[:, j, :])
                    Ptn = R.Pt.next()
                    p.copy("act", Ptn, f2(pp[:, 0:256]))
                    Pn = None
                    if k < 3:
                        pq = self.ps()
                        for j in range(2):
                            p.matmul(J(pq, j), Pt_[:, j, :], P_[:, j, :])
                        Pn = R.P.next()
                        p.copy("dve", Pn, f2(pq[:, 0:256]))
                    return Pn, Ptn

                def lvl_b(Ptn, Y):
                    py = self.ps()
                    for j in range(2):
                        p.matmul(J(py, j), Ptn[:, j, :], Y[:, j, :])
                    Yn = R.Y.next()
                    p.tensor_tensor("dve", Yn, f2(py[:, 0:256]), Y, A)
                    return Yn

                Pn, Ptn = lvl_a(1, P_, Pt_)
                yield
                for k in range(1, 4):
                    Y = lvl_b(Ptn, Y)
                    if k < 3:
                        Pn, Ptn = lvl_a(k + 1, Pn, Ptn)
                    yield
                for lv in range(3):
                    ptb = self.ps().bitcast(BF16)
                    for j in range(2):
                        p.transpose(J(ptb, j), Y[:, j, :], self.ident_b)
                    Bt = R.Pt.next()
                    p.copy("act", Bt, f2(ptb[:, 0:256]))
                    pz = self.ps()
                    for j in range(2):
                        p.matmul(J(pz, j), Ets[lv][:, j, :], Y[:, j, :])
                    Z = R.P.next()
                    p.copy("dve", Z, f2(pz[:, 0:256]))
                    yield
                    pw = self.ps()
                    for j in range(2):
                        p.matmul(J(pw, j), Bt[:, j, :], Z[:, j, :])
                    Yn = HBc['X'] if lv == 2 else R.Y.next()
                    p.tensor_tensor("dve", Yn, Y, f2(pw[:, 0:256]), SUB)
                    Y = Yn
                    yield
                X = Y
                pw = self.ps()
                for j in range(2):
                    p.matmul(J(pw, j), kg[:, j, :], X[:, j, :])
                nw = HBc['nw']
                p.activation(nw, f2(pw[:, 0:256]), AF.Copy, scale=-1.0)
                hand[ch] = (X, vt, nw, kd, attn, qd)
                yield

            def stageB(tt=tt, hand=hand, o_t=o_t):
                for ch in range(4):
                    n = tt * 4 + ch
                    sl = slice(ch * 128, (ch + 1) * 128)
                    cjs = [n * 16 + 2 * g + j for j in (0, 1)]
                    col = lambda t, j: t[:, cjs[j]:cjs[j] + 1]
                    X, vt, nw, kd, attn, qd = hand[ch]
                    pv = self.ps()
                    for j in range(2):
                        p.matmul(J(pv, j), X[:, j, :], vt[:, j, :], start=True, stop=False)
                        p.matmul(J(pv, j), nw[:, j, :], Sb[:, j, :], start=False, stop=True)
                    vn = RS[ch].vn.next()
                    for j in range(2):
                        p.tensor_scalar("dve", vn[:, j, :], J(pv, j), col(T.beta, j), None, op0=M)
                    yield
                    po = self.ps()
                    pS = self.ps()
                    for j in range(2):
                        p.matmul(J(pS, j), kd[:, j, :], vn[:, j, :])
                    for j in range(2):
                        p.matmul(J(po, j), Sb[:, j, :], qd[:, j, :], start=True, stop=False)
                        p.matmul(J(po, j), vn[:, j, :], attn[:, j, :], start=False, stop=True)
                    for j in range(2):
                        p.stt(Sb[:, j, :], Sf[:, j, :], col(T.egl, j), J(pS, j), M, A)
                    for j in range(2):
                        p.stt(Sf[:, j, :], Sf[:, j, :], col(T.egl, j), J(pS, j), M, A)
                    p.copy("act", o_t[:, :, sl], f2(po[:, 0:256]))
                    yield

            def gating(tt=tt, o_t=o_t, zs=zs):
                t0g_ = tt * NT
                sq2 = sqr.next()
                p.activation(sq2, o_t, AF.Square)
                ob = obr.next()
                for j in range(2):
                    pn = self.ps()
                    p.matmul(pn, self.ones_f, sq2[:, j, :])
                    rs = rsr.next()
                    p.activation(rs, pn, AF.Ln, scale=1.0 / 128, bias=self.eps_c)
                    p.activation(rs, rs, AF.Exp, scale=-0.5)
                    p.stt(o_t[:, j, :], o_t[:, j, :], og[:, 0:1], rs, M, M)
                    p.tensor_tensor("pool", ob[:, j, :], o_t[:, j, :], zs[:, j, :], M)
                p.dma(p.sp, View(oT_d, oT_d.ap[tt][:, 2 * g:2 * g + 2, :]), ob)

            gens = [stageA(ch) for ch in range(4)]
            if pending is not None:
                gens.append(pending[0]())
            if tt + 1 < ntt:
                gens.append(front1(tt + 1))
            while gens:
                for gnr in list(gens):
                    try:
                        next(gnr)
                    except StopIteration:
                        gens.remove(gnr)
            if pending is not None:
                pending[1]()
            pending = (stageB, gating)
            if tt == ntt - 1:
                for _ in pending[0]():
                    pass
                pending[1]()
                pending = None
    self.pop()


KB.gb_phase = _gb_phase
KB.gdn_phase = _gdn_phase


def _kv_phase(self, hkT_d, wk_d, wv_d, Kd, Vd):
    p = self.p
    self.push()
    wk = p.sbuf([128, 16, 1024], BF16, "wk")
    wv = p.sbuf([128, 16, 1024], BF16, "wv")
    for (w, wd) in ((wk, wk_d), (wv, wv_d)):
        v = wd.ap.rearrange("(c p) n -> p c n", p=128)
        for q4 in range(8):
            p.dma(p.pool, w[:, q4 * 2:(q4 + 1) * 2, :], View(wd, v[:, q4 * 2:(q4 + 1) * 2, :]))
    hr = Rot(p, "kh", [128, 16, NT], BF16, 2)
    ksr = Rot(p, "kst", [128, 8, NT], BF16, 2)
    vsr = Rot(p, "vst", [128, 4, 1024], BF16, 2)
    kv = Kd.ap.rearrange("i e t -> e i t")
    vv = Vd.ap.rearrange("(n p) f -> p n f", p=128)
    for tt in range(NTT):
        t0 = tt * NT
        h_t = hr.next()
        for q4 in range(2):
            p.dma(self.ldq(), h_t[:, q4 * 8:(q4 + 1) * 8, :], View(hkT_d, hkT_d.ap[tt][:, q4 * 8:(q4 + 1) * 8, :]))
        kst = ksr.next()
        for i in range(8):
            ps = self.ps()
            for c in range(16):
                p.matmul(ps, wk[:, c, i * 128:(i + 1) * 128], h_t[:, c, :], start=(c == 0), stop=(c == 15))
            p.copy("act" if i % 2 else "dve", kst[:, i, :], ps)
        p.dma(p.sp, View(Kd, kv[:, :, t0:t0 + NT]), kst)
        vst = vsr.next()
        for ch in range(4):
            for half in range(2):
                ps = self.ps()
                for c in range(16):
                    p.matmul(ps, h_t[:, c, ch * 128:(ch + 1) * 128], wv[:, c, half * 512:(half + 1) * 512],
                             start=(c == 0), stop=(c == 15))
                p.copy("act" if half else "dve", vst[:, ch, half * 512:(half + 1) * 512], ps)
        p.dma(p.sp, View(Vd, vv[:, tt * 4:(tt + 1) * 4, :]), vst)
    self.pop()


DILS = (1, 4, 16)


def _attn_phase(self, hT_d, wq_d, Kd, Vd, bias_d, mask_d, oT_d, Od, nheads=8):
    p = self.p
    self.push()
    M, A = ALU.mult, ALU.add
    biasm = p.sbuf([128, 3, 256], F32, "biasm")
    bias_v = bias_d.ap.rearrange("p (g i b) -> p g i b", g=3, b=256)
    msk = p.sbuf([128, 256], F32, "msk")
    p.dma(p.sp, msk, mask_d)
    wr = Rot(p, "aw", [128, 16, 512], BF16, 2)
    hr = Rot(p, "ah", [128, 16, NT], BF16, 2)
    knat = p.sbuf([128, S], BF16, "knat")
    kp4 = p.sbuf([128, S], BF16, "kp4")
    kp16 = p.sbuf([128, S], BF16, "kp16")
    kps = (knat, kp4, kp16)
    vps = [p.sbuf([128, 32, 128], BF16, f"vp{g}") for g in range(3)]
    qps = [p.sbuf([128, S], BF16, f"qp{g}") for g in range(3)]
    zs = p.sbuf([128, S], F32, "azs")
    ssp = [[p.sbuf([128, 256], F32, f"as{pp_}{b_}") for b_ in range(4)] for pp_ in range(2)]
    pbp = [[p.sbuf([128, 256], BF16, f"apb{pp_}{b_}") for b_ in range(4)] for pp_ in range(2)]
    ptp_ = [p.sbuf([128, 4, 2, 128], BF16, f"apt{pp_}") for pp_ in range(2)]
    oep = [p.sbuf([128, 4, 132], F32, f"aoe{pp_}") for pp_ in range(2)]
    nmp = [p.sbuf([128, 4], F32, f"anm{pp_}") for pp_ in range(2)]
    ldr = Rot(p, "ald", [128, 3, 132], F32, 3)
    smr = Rot(p, "asm", [128, 16], F32, 3)
    acr = Rot(p, "aac", [128, 128], F32, 3)
    obr = Rot(p, "aob", [128, NT], BF16, 2)
    QS = float(128.0 ** -0.5)

    for i in range(nheads):
        w = wr.next()
        wv = wq_d.ap[i].rearrange("(c p) n -> p c n", p=128)
        for q4 in range(4):
            p.dma(p.pool, w[:, q4 * 4:(q4 + 1) * 4, :], View(wq_d, wv[:, q4 * 4:(q4 + 1) * 4, :]))
        for q4 in range(4):
            p.dma(self.ldq(), knat[:, q4 * 1024:(q4 + 1) * 1024], View(Kd, Kd.ap[i][:, q4 * 1024:(q4 + 1) * 1024]))
        for g, d in ((1, 4), (2, 16)):
            p.copy("pool", kps[g][:, :].rearrange("p (r u) -> p r u", r=d),
                   knat[:, :].rearrange("p (u r) -> p r u", r=d))
        vcol = Vd.ap[:, i * 128:(i + 1) * 128]
        for g, d in enumerate(DILS):
            nb = 32 // d
            if d == 1:
                src = vcol.rearrange("(ub a) f -> a ub f", a=128)
                for q4 in range(4):
                    p.dma(self.ldq(), vps[g][:, q4 * 8:(q4 + 1) * 8, :], View(Vd, src[:, q4 * 8:(q4 + 1) * 8, :]))
            else:
                src = vcol.rearrange("(ub a r) f -> a r ub f", a=128, r=d)
                dst = vps[g][:, :, :].rearrange("a (r ub) f -> a r ub f", r=d)
                for r in range(d):
                    p.dma(self.ldq(), dst[:, r], View(Vd, src[:, r]))
        for tt in range(NTT):
            t0 = tt * NT
            h_t = hr.next()
            for q4 in range(2):
                p.dma(self.ldq(), h_t[:, q4 * 8:(q4 + 1) * 8, :], View(hT_d, hT_d.ap[tt][:, q4 * 8:(q4 + 1) * 8, :]))
            for blk in range(4):
                ps = self.ps()
                for c in range(16):
                    p.matmul(ps, w[:, c, blk * 128:(blk + 1) * 128], h_t[:, c, :], start=(c == 0), stop=(c == 15))
                if blk == 3:
                    p.activation(zs[:, t0:t0 + NT], ps, AF.Silu)
                else:
                    d = DILS[blk]
                    if d == 1:
                        p.activation(qps[blk][:, t0:t0 + NT], ps, AF.Copy, scale=QS)
                    else:
                        w_ = NT // d
                        dst = qps[blk][:, :].rearrange("p (r u) -> p r u", r=d)[:, :, w_ * tt:w_ * (tt + 1)]
                        src = ps[:, :].rearrange("p (a r) -> p r a", r=d)
                        p.activation(dst, src, AF.Copy, scale=QS)
        p.dma(p.sp, biasm, View(bias_d, bias_v[:, :, i, :]))
        for g_ in range(3):
            p.tensor_tensor("pool", biasm[:, g_, :], biasm[:, g_, :], msk, A)

        def grp(g, j0, par):
            d = DILS[g]
            nb = 32 // d
            qp, kp, vp = qps[g], kps[g], vps[g]
            bm = biasm[:, g, :]
            odv = Od.ap[g, i]
            if d == 1:
                odv = odv.rearrange("(ub a) c -> a ub c", a=128)
            else:
                odv = odv.rearrange("(ub a r) c -> a r ub c", a=128, r=d)
            banks = self.psb[4 * par:4 * par + 4]
            blks = list(range(j0, j0 + 4))
            los = [128 if (j % nb == 0) else 0 for j in blks]
            pss = []
            for b, j in enumerate(blks):
                ps_s = banks[b // 2][:, (b % 2) * 256:(b % 2) * 256 + 256]
                lo = los[b]
                p.matmul(ps_s[:, lo:256], qp[:, j * 128:(j + 1) * 128], kp[:, j * 128 - (128 - lo):(j + 1) * 128])
                pss.append(ps_s)
            yield
            oe = oep[par]
            nm = nmp[par]
            ss = ssp[par]
            for b, j in enumerate(blks):
                lo = los[b]
                p.tensor_tensor("dve", ss[b][:, lo:256], pss[b][:, lo:256], bm[:, lo:256], A)
                p.reduce("dve", oe[:, b, 128:129], ss[b][:, lo:256], ALU.max)
            p.tensor_scalar("dve", nm, oe[:, :, 128], -1.0, None, op0=M)
            yield
            pbs = pbp[par]
            for b, j in enumerate(blks):
                lo = los[b]
                p.activation(pbs[b][:, lo:256], ss[b][:, lo:256], AF.Exp, bias=nm[:, b:b + 1],
                             accum_out=oe[:, b, 129:130])
            yield
            ptp = banks[2].bitcast(BF16)
            for b, j in enumerate(blks):
                for kb in range(los[b] // 128, 2):
                    c0 = (b * 2 + kb) * 128
                    p.transpose(ptp[:, c0:c0 + 128], pbs[b][:, kb * 128:(kb + 1) * 128], self.ident_b)
            pt = ptp_[par]
            if all(lo == 0 for lo in los):
                p.copy("dve", pt, ptp[:, 0:1024].rearrange("p (a b c) -> p a b c", a=4, b=2))
            else:
                for b in range(4):
                    kb0 = los[b] // 128
                    p.copy("dve", pt[:, b, kb0:2, :],
                           ptp[:, (b * 2 + kb0) * 128:(b * 2 + 2) * 128].rearrange("p (b c) -> p b c", c=128))
            yield
            po = banks[3]
            for b, j in enumerate(blks):
                kb0 = los[b] // 128
                for kb in range(kb0, 2):
                    p.matmul(po[:, b * 128:(b + 1) * 128], pt[:, b, kb, :], vp[:, j - 1 + kb, :],
                             start=(kb == kb0), stop=(kb == 1))
            p.copy("act", oe[:, :, 0:128], po[:, :].rearrange("p (a b) -> p a b", b=128))
            for b, j in enumerate(blks):
                r, ub = j // nb, j % nb
                dstv = odv[:, ub, 0:130] if d == 1 else odv[:, r, ub, 0:130]
                p.dma(p.sp, View(Od, dstv), oe[:, b, 0:130])
            yield

        todo = [(g, j0) for g in range(3) for j0 in range(0, 32, 4)]
        active = []
        kcnt = 0
        first = True
        while todo or active:
            while len(active) < 2 and todo:
                g_, j0_ = todo.pop(0)
                gn = grp(g_, j0_, kcnt % 2)
                kcnt += 1
                active.append(gn)
                if first:
                    first = False
                    next(gn)
                    next(gn)
                    next(gn)
            for gn in list(active):
                try:
                    next(gn)
                except StopIteration:
                    active.remove(gn)
        for t4 in range(NTT):
            pso = self.ps()
            for bq in range(4):
                nblk = t4 * 4 + bq
                ld = ldr.next()
                src = Od.ap[:, i, nblk * 128:(nblk + 1) * 128, :].rearrange("g a c -> a g c")
                p.dma(self.ldq(), ld, View(Od, src))
                sm = smr.next()
                p.reduce("dve", sm[:, 0:1], ld[:, :, 128], ALU.max)
                p.tensor_scalar("dve", sm[:, 1:2], sm[:, 0:1], -1.0, None, op0=M)
                p.activation(sm[:, 2:5], ld[:, :, 128], AF.Exp, bias=sm[:, 1:2])
                p.tensor_tensor("dve", sm[:, 10:13], sm[:, 2:5], ld[:, :, 129], M)
                p.reduce("dve", sm[:, 5:6], sm[:, 10:13], ALU.add)
                p.reciprocal(sm[:, 6:7], sm[:, 5:6])
                p.tensor_scalar("dve", sm[:, 7:10], sm[:, 2:5], sm[:, 6:7], None, op0=M)
                acc = acr.next()
                p.tensor_scalar("dve", acc, ld[:, 0, 0:128], sm[:, 7:8], None, op0=M)
                p.stt(acc, ld[:, 1, 0:128], sm[:, 8:9], acc, M, A)
                p.stt(acc, ld[:, 2, 0:128], sm[:, 9:10], acc, M, A)
                p.transpose(pso[:, bq * 128:(bq + 1) * 128], acc, self.ident_f)
            ob = obr.next()
            p.tensor_tensor("dve", ob, pso, zs[:, t4 * NT:(t4 + 1) * NT], M)
            p.dma(p.sp, View(oT_d, oT_d.ap[t4][:, i, :]), ob)
    self.pop()


KB.kv_phase = _kv_phase
KB.attn_phase = _attn_phase


import ml_dtypes as _mld

_BF = _mld.bfloat16
_PROGS = {}


def _consts_np():
    i = np.arange(128)
    ident = np.eye(128, dtype=np.float32)
    ones = np.ones((128, 128), np.float32)
    U = (i[:, None] <= i[None, :]).astype(np.float32)
    SU = (i[None, :] > i[:, None]).astype(np.float32)
    UU = (i[None, :] >= i[:, None]).astype(np.float32)
    return np.concatenate([ident, ones, U, SU, UU], axis=1)


def _consts2_np():
    i = np.arange(128)
    bm = lambda s: (i[:, None] // s == i[None, :] // s)
    ms = [bm(16), bm(32) & ~bm(16), bm(64) & ~bm(32), ~bm(64)]
    return np.concatenate([np.concatenate([m, m], axis=1) for m in ms], axis=1).astype(np.float32)


def _fm(v):
    return np.ascontiguousarray(np.asarray(v, np.float32).reshape(-1, 128).T)


def _t5_bucket_np(dist):
    n = np.maximum(dist, 0)
    max_exact = 16
    large = max_exact + (np.log(np.maximum(n, 1).astype(np.float32) / np.float32(max_exact))
                         / np.float32(np.log(2048 / max_exact)) * np.float32(32 - max_exact)).astype(np.int32)
    large = np.minimum(large, 31)
    return np.where(n < max_exact, n, large)


def _bias_np(rel_bias, hh):
    q = np.arange(128)[:, None]
    k = np.arange(256)[None, :]
    rel = q + 128 - k
    valid = (rel >= 0) & (rel <= 128)
    out = np.zeros((128, 3, 8, 256), np.float32)
    for g, d in enumerate((1, 4, 16)):
        bk = _t5_bucket_np(rel * d)
        for i in range(8):
            out[:, g, i, :] = np.where(valid, rel_bias[bk, g * 16 + hh * 8 + i], 0.0)
    mask = np.where(valid, 0.0, -30000.0).astype(np.float32)
    return out.reshape(128, 3 * 8 * 256), mask


def _prep_gdn(inp, l, hh):
    w_in = inp["w_in_a"][l]
    win = np.empty((8, 2048, 768), np.float32)
    convw = np.empty((128, 8, 4, 4), np.float32)
    cwl = inp["conv_w_a"][l]
    for g in range(8):
        kh = hh * 8 + g
        cols = [(kh * 128), (2048 + kh * 128), (4096 + (2 * kh) * 128), (4096 + (2 * kh + 1) * 128),
                (8192 + (2 * kh) * 128), (8192 + (2 * kh + 1) * 128)]
        for i, c0 in enumerate(cols):
            win[g, :, i * 128:(i + 1) * 128] = w_in[:, c0:c0 + 128]
        for i, c0 in enumerate(cols[:4]):
            convw[:, g, i, :] = cwl[:, c0:c0 + 128].T
    vh = np.arange(hh * 16, hh * 16 + 16)
    wba = np.concatenate([w_in[:, 12288 + vh], w_in[:, 12320 + vh]], axis=1)
    rep = lambda v: np.ascontiguousarray(np.broadcast_to(np.tile(v[vh], 32)[None, :], (128, 512))).astype(np.float32)
    return dict(win=win, wba=np.ascontiguousarray(wba), convw=convw.reshape(128, 128),
                ogain=np.asarray(inp["o_norm_a"][l], np.float32).reshape(128, 1).copy(),
                alog=rep(inp["a_log"][l]), dtb=rep(inp["dt_bias"][l]))


def _prep_attn(inp, j, hh):
    w = inp["w_in_b"][j]
    wq = np.empty((8, 2048, 512), np.float32)
    for i in range(8):
        hd = hh * 8 + i
        for g in range(3):
            wq[i, :, g * 128:(g + 1) * 128] = w[:, g * 2048 + hd * 128: g * 2048 + (hd + 1) * 128]
        wq[i, :, 384:512] = w[:, 6144 + hd * 128: 6144 + (hd + 1) * 128]
    return wq


def _prog(key, fn, *a):
    if key not in _PROGS:
        _PROGS[key] = fn(*a)
    return _PROGS[key]


def _build_fused():
    nc = bass.Bass("TRN2", target_bir_lowering=False)
    kb = KB(nc)
    p = kb.p
    consts = kb.ext_in("consts", [128, 640])
    kb.load_consts(consts)
    consts2 = kb.ext_in("consts2", [128, 1024])
    kb.load_consts2(consts2)
    xT = kb.ext_in("xT", [NTT, 128, 16, NT])
    cT = kb.ext_in("cT", [128, 16])
    wmod = kb.ext_in("wmod", [4, 2048, 6144])
    bmodT = kb.ext_in("bmodT", [4, 128, 48])
    gainT = kb.ext_in("gainT", [4, 128, 16])
    win = kb.ext_in("win", [2, 2, 8, 2048, 768])
    wba = kb.ext_in("wba", [2, 2, 2048, 32])
    convw = kb.ext_in("convw", [2, 2, 128, 128])
    ogain = kb.ext_in("ogain", [2, 128, 1])
    alog = kb.ext_in("alog", [2, 2, 128, 512])
    dtb = kb.ext_in("dtb", [2, 2, 128, 512])
    wouta = kb.ext_in("wouta", [2, 4096, 2048])
    wkvmod = kb.ext_in("wkvmod", [2048, 4096])
    bkvmodT = kb.ext_in("bkvmodT", [128, 32])
    kvgainT = kb.ext_in("kvgainT", [128, 16])
    wk = kb.ext_in("wk", [2, 2048, 1024])
    wv = kb.ext_in("wv", [2, 2048, 1024])
    wq = kb.ext_in("wq", [2, 2, 8, 2048, 512])
    bias = kb.ext_in("bias", [2, 128, 3 * 8 * 256])
    mask = kb.ext_in("mask", [128, 256])
    woutb = kb.ext_in("woutb", [2, 2048, 2048])
    fgT = kb.ext_in("fgT", [128, 16])
    out = kb.ext_out("outT", [NTT, 128, 16, NT])

    hT = p.dram("hT", [NTT, 128, 16, NT], BF16)
    hkT = p.dram("hkT", [NTT, 128, 16, NT], BF16)
    oTa = p.dram("oTa", [NTT, 128, 32, NT], BF16)
    oTb = p.dram("oTb", [NTT, 128, 16, NT], BF16)
    xA = p.dram("xA", [NTT, 128, 16, NT], F32)
    xB = p.dram("xB", [NTT, 128, 16, NT], F32)
    Kd = [p.dram(f"Kd{h}", [8, 128, S], BF16) for h in range(2)]
    Vd = [p.dram(f"Vd{h}", [S, 1024], BF16) for h in range(2)]
    Od = p.dram("Od", [3, 8, S, 132], F32)
    sub = lambda buf, *idx: Buf(buf.ap[idx] if len(idx) > 1 else buf.ap[idx[0]], buf.name + "_s")

    fg = p.sbuf([128, 16], F32, "fg")
    p.dma(p.sp, fg, fgT)
    x_cur = xT
    x_next = [xA, xB, xA, out]
    for l in range(4):
        kb.push()
        mods = p.sbuf([128, 48], F32, "mods")
        gn = p.sbuf([128, 16], F32, "gn")
        Acol = p.sbuf([128, 16], F32, "Acol")
        p.dma(p.sp, gn, sub(gainT, l))
        kb.mods_phase(cT, sub(wmod, l), sub(bmodT, l), 6144, mods)
        p.stt(Acol, mods[:, 16:32], 1.0, gn, ALU.add, ALU.mult)
        if l < 2:
            kb.push()
            bas = [p.sbuf([128, 2, 512], F32, "ba") for _ in range(2)]
            kb.norm_phase(x_cur, hT, Acol, mods[:, 0:16], [sub(wba, l, 0), sub(wba, l, 1)], bas)
            for hh in range(2):
                kb.push()
                T = kb.gb_phase(bas[hh], sub(alog, l, hh), sub(dtb, l, hh))
                kb.gdn_phase(hT, sub(win, l, hh), sub(convw, l, hh), sub(ogain, l),
                             Buf(oTa.ap[:, :, 16 * hh:16 * (hh + 1), :], "oTa_h"), T)
                kb.pop()
            kb.pop()
            kb.outproj_phase(oTa, 32, sub(wouta, l), x_cur, x_next[l], mods[:, 32:48])
        else:
            j = l - 2
            if j == 0:
                kb.push()
                kvm = p.sbuf([128, 32], F32, "kvm")
                kg = p.sbuf([128, 16], F32, "kvg")
                Akv = p.sbuf([128, 16], F32, "Akv")
                p.dma(p.sp, kg, kvgainT)
                kb.mods_phase(cT, wkvmod, bkvmodT, 4096, kvm)
                p.stt(Akv, kvm[:, 16:32], 1.0, kg, ALU.add, ALU.mult)
                kb.norm_phase(x_cur, hkT, Akv, kvm[:, 0:16])
                for hh in range(2):
                    kb.kv_phase(hkT, sub(wk, hh), sub(wv, hh), Kd[hh], Vd[hh])
                kb.pop()
            kb.norm_phase(x_cur, hT, Acol, mods[:, 0:16])
            for hh in range(2):
                kb.attn_phase(hT, sub(wq, j, hh), Kd[hh], Vd[hh], sub(bias, hh), mask,
                              Buf(oTb.ap[:, :, 8 * hh:8 * (hh + 1), :], "oTb_h"), Od)
            kb.outproj_phase(oTb, 16, sub(woutb, j), x_cur, x_next[l], mods[:, 32:48],
                             final_gain=(fg if l == 3 else None))
        kb.pop()
        x_cur = x_next[l]
    p.finish()
    p.close()
    return nc


def kernel(**inputs):
    inp = {k: np.asarray(v) for k, v in inputs.items()}
    x = inp["x"].astype(np.float32)
    c = inp["c"].astype(np.float32)
    f32 = lambda a: np.ascontiguousarray(a, dtype=np.float32)
    gd = [[_prep_gdn(inp, l, hh) for hh in range(2)] for l in range(2)]
    stack = lambda key: f32(np.stack([np.stack([gd[l][hh][key] for hh in range(2)]) for l in range(2)]))
    biases = [_bias_np(inp["rel_bias"].astype(np.float32), hh) for hh in range(2)]
    wkv = inp["w_kv"].astype(np.float32)
    shared = {
        "consts": _consts_np(), "consts2": _consts2_np(),
        "wmod": f32(inp["w_mod"]),
        "bmodT": f32(np.stack([_fm(inp["b_mod"][l]) for l in range(4)])),
        "gainT": f32(np.stack([_fm(inp["norm_gain"][l]) for l in range(4)])),
        "win": stack("win"), "wba": stack("wba"), "convw": stack("convw"),
        "ogain": f32(np.stack([gd[l][0]["ogain"] for l in range(2)])),
        "alog": stack("alog"), "dtb": stack("dtb"),
        "wouta": f32(inp["w_out_a"]),
        "wkvmod": f32(inp["w_kv_mod"]), "bkvmodT": _fm(inp["b_kv_mod"]), "kvgainT": _fm(inp["kv_gain"]),
        "wk": f32(np.stack([wkv[:, hh * 1024:(hh + 1) * 1024] for hh in range(2)])),
        "wv": f32(np.stack([wkv[:, 2048 + hh * 1024:2048 + (hh + 1) * 1024] for hh in range(2)])),
        "wq": f32(np.stack([np.stack([_prep_attn(inp, j, hh) for hh in range(2)]) for j in range(2)])),
        "bias": f32(np.stack([biases[hh][0] for hh in range(2)])), "mask": biases[0][1],
        "woutb": f32(inp["w_out_b"]), "fgT": _fm(inp["final_gain"]),
    }
    nc = _prog("F", _build_fused)
    in_maps = []
    for core in range(8):
        b = core // 2
        m = dict(shared)
        m["xT"] = np.ascontiguousarray(x[b].reshape(NTT, NT, 16, 128).transpose(0, 3, 2, 1))
        m["cT"] = _fm(c[b])
        in_maps.append(m)
    res = run_bass_kernel_spmd(nc, in_maps, core_ids=list(range(8))).results
    out = np.empty((4, S, D), np.float32)
    for b in range(4):
        out[b] = np.asarray(res[2 * b]["outT"]).transpose(0, 3, 2, 1).reshape(S, D)
    return out
```
